# Optimizing a Trainium2 kernel written in Bass

```python
import math
import jax, jax.numpy as jnp
from jax import lax
import numpy as np

D_MODEL = 1024
BATCH = 8
SEQ = 4096
DEPTH = 4
DEC_BATCH = 16
DEC_SEQ = 2048
PAST_LEN = 128

HEAD_DIM = 64
N_ATT_HEADS = 8
D_ATT = N_ATT_HEADS * HEAD_DIM
D_CONV = D_MODEL - D_ATT
D_IN = 3 * D_ATT + 2 * D_CONV
CONV_KERNEL = 31
FFN_CONV_KERNEL = 3
D_FF = 2816
DILATED_PATTERNS = ((128, 1), (512, 4), (2048, 16))
N_BUCKETS = 32
REL_MAX_DIST = 1024
EPS = 1e-6
NEG = -1e30
ATT_SCALE = 1.0 / math.sqrt(HEAD_DIM)

kernel_name = "hybrid_dilated_conformer_encoder"


def rms_norm(x, g):
    x32 = x.astype(jnp.float32)
    y = x32 * lax.rsqrt(jnp.mean(x32 * x32, axis=-1, keepdims=True) + EPS)
    return y.astype(x.dtype) * g


def layer_norm(x, g, b):
    x32 = x.astype(jnp.float32)
    mu = jnp.mean(x32, axis=-1, keepdims=True)
    var = jnp.mean(jnp.square(x32 - mu), axis=-1, keepdims=True)
    return ((x32 - mu) * lax.rsqrt(var + EPS)).astype(x.dtype) * g + b


def depthwise_conv(x, w):
    k = w.shape[0]
    pad = k // 2
    return lax.conv_general_dilated(
        x, w.astype(x.dtype)[:, None, :], window_strides=(1,), padding=[(pad, pad)],
        dimension_numbers=("NWC", "WIO", "NWC"), feature_group_count=x.shape[-1])


def t5_bucket_np(rel):
    n = -rel
    half = N_BUCKETS // 2
    ret = (n < 0).astype(np.int32) * half
    n = np.abs(n)
    max_exact = half // 2
    large = max_exact + (np.log(np.maximum(n, 1) / max_exact) / np.log(REL_MAX_DIST / max_exact)
                         * (half - max_exact)).astype(np.int32)
    large = np.minimum(large, half - 1)
    return (ret + np.where(n < max_exact, n, large)).astype(np.int32)


def dilated_branch(q, k, v, rel_bias, window, dilation):
    B, S, H, Dh = q.shape
    r = window // (2 * dilation)
    L = S // dilation
    N = B * dilation

    def to_sub(t):
        return t.reshape(B, L, dilation, H, Dh).transpose(0, 2, 1, 3, 4).reshape(N, L, H, Dh)

    nb = -(-L // r)
    Lp = nb * r
    qb = jnp.pad(to_sub(q), ((0, 0), (0, Lp - L), (0, 0), (0, 0))).reshape(N, nb, r, H, Dh)

    def key_blocks(t):
        tp = jnp.pad(to_sub(t), ((0, 0), (r, Lp - L + r), (0, 0), (0, 0))).reshape(N, nb + 2, r, H, Dh)
        return jnp.concatenate([tp[:, :-2], tp[:, 1:-1], tp[:, 2:]], axis=2)

    kb = key_blocks(k)
    vb = key_blocks(v)

    t_idx = np.arange(r)[:, None]
    u_idx = np.arange(3 * r)[None, :]
    rel = u_idx - r - t_idx
    bias = rel_bias[t5_bucket_np(rel * dilation)]
    blk = np.arange(nb)[:, None, None]
    kpos = blk * r + u_idx[None] - r
    valid = (np.abs(rel)[None] <= r) & (kpos >= 0) & (kpos < L)

    s = jnp.einsum("nbqhd,nbkhd->nbhqk", qb, kb, preferred_element_type=jnp.float32)
    s = s * ATT_SCALE + jnp.transpose(bias, (2, 0, 1)).astype(jnp.float32)
    s = jnp.where(valid[None, :, None], s, NEG)
    m = jnp.max(s, axis=-1, keepdims=True)
    p = jnp.exp(s - m)
    den = jnp.sum(p, axis=-1, keepdims=True)
    o = jnp.einsum("nbhqk,nbkhd->nbqhd", p, vb.astype(jnp.float32)) / jnp.transpose(den, (0, 1, 3, 2, 4))
    lse = jnp.transpose((m + jnp.log(den))[..., 0], (0, 1, 3, 2))

    o = o.reshape(N, Lp, H, Dh)[:, :L].reshape(B, dilation, L, H, Dh).transpose(0, 2, 1, 3, 4).reshape(B, S, H, Dh)
    lse = lse.reshape(N, Lp, H)[:, :L].reshape(B, dilation, L, H).transpose(0, 2, 1, 3).reshape(B, S, H)
    return o, lse


def mixer(h, rel_bias, w_in, q_g, k_g, dw_w, dw_b, ln_g, ln_b, w_out):
    B, S, _ = h.shape
    proj = h @ w_in
    q, k, v, cv, cg = jnp.split(proj, [D_ATT, 2 * D_ATT, 3 * D_ATT, 3 * D_ATT + D_CONV], axis=-1)
    q = rms_norm(q.reshape(B, S, N_ATT_HEADS, HEAD_DIM), q_g)
    k = rms_norm(k.reshape(B, S, N_ATT_HEADS, HEAD_DIM), k_g)
    v = v.reshape(B, S, N_ATT_HEADS, HEAD_DIM)
    outs, lses = [], []
    for window, dilation in DILATED_PATTERNS:
        o, l = dilated_branch(q, k, v, rel_bias, window, dilation)
        outs.append(o)
        lses.append(l)
    wts = jax.nn.softmax(jnp.stack(lses), axis=0)
    att = jnp.sum(wts[..., None] * jnp.stack(outs), axis=0).reshape(B, S, D_ATT).astype(h.dtype)
    u = cv * jax.nn.sigmoid(cg)
    u = depthwise_conv(u, dw_w) + dw_b
    u = jax.nn.silu(layer_norm(u, ln_g, ln_b))
    return jnp.concatenate([att, u], axis=-1) @ w_out


def conv_ffn(h, w_up, dw_w, w_down):
    a, g = jnp.split(h @ w_up, 2, axis=-1)
    g = depthwise_conv(g, dw_w)
    return (a * jax.nn.gelu(g)) @ w_down


def trunk(x, c, rel_bias, norm1_g, norm2_g, w_ada, b_ada, w_in, q_norm_g, k_norm_g,
          conv_dw_w, conv_dw_b, conv_ln_g, conv_ln_b, w_out, w_up, ffn_dw_w, w_down):
    sc = jax.nn.silu(c)
    for l in range(DEPTH):
        mod = (sc @ w_ada[l] + b_ada[l])[:, None, :]
        sh1, s1, g1, sh2, s2, g2 = jnp.split(mod, 6, axis=-1)
        h = rms_norm(x, norm1_g[l]) * (1 + s1) + sh1
        x = x + g1 * mixer(h, rel_bias, w_in[l], q_norm_g[l], k_norm_g[l], conv_dw_w[l],
                           conv_dw_b[l], conv_ln_g[l], conv_ln_b[l], w_out[l])
        h = rms_norm(x, norm2_g[l]) * (1 + s2) + sh2
        x = x + g2 * conv_ffn(h, w_up[l], ffn_dw_w[l], w_down[l])
    return x


def setup_inputs(seed: int = 0) -> dict:
    key = jax.random.key(seed)
    ks = jax.random.split(key, 20)
    f32 = jnp.float32
    nrm = lambda k, shape, s: jax.random.normal(k, shape, f32) * s
    return {
        "x_prompt": nrm(ks[0], (BATCH, SEQ, D_MODEL), 1.0),
        "x_sample": nrm(ks[1], (DEC_BATCH, DEC_SEQ, D_MODEL), 1.0),
        "c_prompt": nrm(ks[2], (BATCH, D_MODEL), 1.0),
        "c_sample": nrm(ks[3], (DEC_BATCH, D_MODEL), 1.0),
        "rel_bias": nrm(ks[4], (N_BUCKETS, N_ATT_HEADS), 0.5),
        "norm1_g": 1.0 + nrm(ks[5], (DEPTH, D_MODEL), 0.05),
        "norm2_g": 1.0 + nrm(ks[6], (DEPTH, D_MODEL), 0.05),
        "w_ada": nrm(ks[7], (DEPTH, D_MODEL, 6 * D_MODEL), 0.5 * D_MODEL ** -0.5),
        "b_ada": nrm(ks[8], (DEPTH, 6 * D_MODEL), 0.02),
        "w_in": nrm(ks[9], (DEPTH, D_MODEL, D_IN), D_MODEL ** -0.5),
        "q_norm_g": 1.0 + nrm(ks[10], (DEPTH, HEAD_DIM), 0.05),
        "k_norm_g": 1.0 + nrm(ks[11], (DEPTH, HEAD_DIM), 0.05),
        "conv_dw_w": nrm(ks[12], (DEPTH, CONV_KERNEL, D_CONV), CONV_KERNEL ** -0.5),
        "conv_dw_b": nrm(ks[13], (DEPTH, D_CONV), 0.02),
        "conv_ln_g": 1.0 + nrm(ks[14], (DEPTH, D_CONV), 0.05),
        "conv_ln_b": nrm(ks[15], (DEPTH, D_CONV), 0.02),
        "w_out": nrm(ks[16], (DEPTH, D_MODEL, D_MODEL), D_MODEL ** -0.5),
        "w_up": nrm(ks[17], (DEPTH, D_MODEL, 2 * D_FF), D_MODEL ** -0.5),
        "ffn_dw_w": nrm(ks[18], (DEPTH, FFN_CONV_KERNEL, D_FF), FFN_CONV_KERNEL ** -0.5),
        "w_down": nrm(ks[19], (DEPTH, D_FF, D_MODEL), D_FF ** -0.5),
    }


def reference(x_prompt, x_sample, c_prompt, c_sample, rel_bias, norm1_g, norm2_g, w_ada, b_ada,
              w_in, q_norm_g, k_norm_g, conv_dw_w, conv_dw_b, conv_ln_g, conv_ln_b, w_out,
              w_up, ffn_dw_w, w_down):
    y_prompt = trunk(x_prompt, c_prompt, rel_bias, norm1_g, norm2_g, w_ada, b_ada, w_in,
                     q_norm_g, k_norm_g, conv_dw_w, conv_dw_b, conv_ln_g, conv_ln_b, w_out,
                     w_up, ffn_dw_w, w_down)
    y_sample = trunk(x_sample, c_sample, rel_bias, norm1_g, norm2_g, w_ada, b_ada, w_in,
                     q_norm_g, k_norm_g, conv_dw_w, conv_dw_b, conv_ln_g, conv_ln_b, w_out,
                     w_up, ffn_dw_w, w_down)
    return (y_prompt, y_sample)
```

```python
import math
import os
from contextlib import ExitStack

_DBG = os.environ.get('DBG_B', 'all')
_LV = {'load': 0, 'qk': 1, 'exp': 2, 'mul': 3, 'pv': 4, 'all': 5}[_DBG]

import numpy as np

import concourse.bass as bass
import concourse.mybir as mybir
from concourse.bass_utils import run_bass_kernel_spmd

F32 = mybir.dt.float32
BF16 = mybir.dt.bfloat16
AF = mybir.ActivationFunctionType
ALU = mybir.AluOpType

D = 1024
DEPTH = 4
HEAD = 64
DFF = 2816
NJ = 22
DIN = 2560
CK = 31
PATTERNS = ((128, 1), (512, 4), (2048, 16))
N_BUCKETS = 32
REL_MAX_DIST = 1024
EPS = 1e-6
NCORES = 8


def _t5_bucket(rel):
    n = -rel
    half = N_BUCKETS // 2
    ret = (n < 0).astype(np.int32) * half
    n = np.abs(n)
    max_exact = half // 2
    large = max_exact + (np.log(np.maximum(n, 1) / max_exact) / np.log(REL_MAX_DIST / max_exact)
                         * (half - max_exact)).astype(np.int32)
    large = np.minimum(large, half - 1)
    return (ret + np.where(n < max_exact, n, large)).astype(np.int32)


def _static_tables():
    oh = np.zeros((32, 6, 256), np.float32)
    vm = np.zeros((8, 6, 256), np.float32)
    for pi, (_, d) in enumerate(PATTERNS):
        for blk in range(2):
            for n in range(255):
                delta = 127 - n
                if blk == 0:
                    rel = delta - 64
                    valid = delta >= 0
                else:
                    rel = delta + 64
                    valid = delta <= 0
                if valid:
                    b = int(_t5_bucket(np.array([rel * d]))[0])
                    oh[b, pi * 2 + blk, n] = 1.0
                    vm[:, pi * 2 + blk, n] = 1.0
    return oh.reshape(32, 1536), vm.reshape(8, 1536)


class Dep:
    __slots__ = ("w", "r")

    def __init__(self):
        self.w = None
        self.r = {}


class EngW:
    def __init__(self, e, sem, name):
        self.e = e
        self.sem = sem
        self.name = name
        self.n = 0
        self.seen = {}


class DSem:
    def __init__(self, sem):
        self.sem = sem
        self.n = 0


class K:
    def __init__(self, nc, es):
        self.nc = nc
        self.es = es
        self.nsem = 0
        self.PE = self._eng(nc.tensor, "pe")
        self.ACT = self._eng(nc.scalar, "act")
        self.DVE = self._eng(nc.vector, "dve")
        self.POOL = self._eng(nc.gpsimd, "pool")
        self.SP = self._eng(nc.sync, "sp")
        self.engs = [self.PE, self.ACT, self.DVE, self.POOL, self.SP]
        self.dsems = []
        self.ddeps = {}
        self.dead = False
        self.dcache = {}

    def _eng(self, e, name):
        return EngW(e, self.es.enter_context(self.nc.semaphore("s_" + name)), name)

    def dsem(self, name):
        if name in self.dcache:
            return self.dcache[name]
        d = DSem(self.es.enter_context(self.nc.semaphore("d_" + name)))
        self.dsems.append(d)
        self.dcache[name] = d
        return d

    def ddep(self, name, idx):
        key = (name, idx)
        if key not in self.ddeps:
            self.ddeps[key] = Dep()
        return self.ddeps[key]

    def _waits(self, E, reads, writes):
        waits = {}

        def need(ev, same_ok):
            if ev is None:
                return
            sem, val, owner = ev
            if owner is E and same_ok:
                return
            key = id(sem)
            if E.seen.get(key, 0) >= val:
                return
            if key not in waits or waits[key][1] < val:
                waits[key] = (sem, val)

        for b in reads:
            need(b.w, False)
        for b in writes:
            need(b.w, True)
            for ev in b.r.values():
                need(ev, True)
        for key, (sem, val) in waits.items():
            E.e.wait_ge(sem, val)
            E.seen[key] = val

    def op(self, E, fn, reads=(), writes=(), signal=True):
        if self.dead:
            return None
        self._waits(E, reads, writes)
        ins = fn(E.e)
        if signal:
            ins.then_inc(E.sem, 1)
            E.n += 1
            tick = E.n
        else:
            tick = E.n + 1
        ev = (E.sem, tick, E)
        for b in reads:
            b.r[id(E)] = ev
        for b in writes:
            b.w = ev
            b.r = {}
        return ins

    def dma(self, Q, out, in_, reads, writes, ds):
        if self.dead:
            return None
        self._waits(Q, reads, writes)
        ins = Q.e.dma_start(out=out, in_=in_)
        ins.then_inc(ds.sem, 16)
        ds.n += 16
        ev = (ds.sem, ds.n, None)
        for b in reads:
            b.r[id(ds)] = ev
        for b in writes:
            b.w = ev
            b.r = {}
        return ins

    def barrier(self):
        if self.dead:
            return
        for E in self.engs:
            for X in self.engs:
                if X is E or X.n == 0:
                    continue
                if E.seen.get(id(X.sem), 0) < X.n:
                    E.e.wait_ge(X.sem, X.n)
                    E.seen[id(X.sem)] = X.n
            for d in self.dsems:
                if d.n and E.seen.get(id(d.sem), 0) < d.n:
                    E.e.wait_ge(d.sem, d.n)
                    E.seen[id(d.sem)] = d.n

    def final_wait(self, E):
        for d in self.dsems:
            if d.n and E.seen.get(id(d.sem), 0) < d.n:
                E.e.wait_ge(d.sem, d.n)
                E.seen[id(d.sem)] = d.n
        for X in self.engs:
            if X is E or X.n == 0:
                continue
            if E.seen.get(id(X.sem), 0) < X.n:
                E.e.wait_ge(X.sem, X.n)
                E.seen[id(X.sem)] = X.n


class _Stop(Exception):
    pass


def build_program(seqs=(4096, 2048, 2048), depth=DEPTH, debug=False, stop=None):
    NSEQ = len(seqs)
    NTOK = sum(seqs)
    SMAX = max(seqs)
    seq_base = [sum(seqs[:i]) for i in range(NSEQ)]
    nc = bass.Bass("TRN2", target_bir_lowering=False)
    skind = "ExternalOutput" if debug else "Internal"

    def din(name, shape, dt=F32):
        return nc.dram_tensor(name, list(shape), dt, kind="ExternalInput")

    def dscr(name, shape, dt):
        return nc.dram_tensor(name, list(shape), dt, kind=skind)

    xin_t = din("xin", [NTOK, D])
    cT_in = din("cT", [128, 8, NSEQ])
    relb_in = din("relb", [32, 8])
    oh_in = din("oh", [32, 1536])
    vm_in = din("vm", [8, 1536])
    ng_in = din("ng", [128, DEPTH, 2, 8])
    badac_in = din("badac", [128, DEPTH, 48])
    bada_in = din("bada", [DEPTH, 6 * D])
    wada_in = din("wada", [DEPTH, 12, 128, 8, 512])
    win_in = din("win", [DEPTH, 128, 8, DIN])
    qkg_in = din("qkg", [128, DEPTH, 2])
    cw_in = din("cw", [128, DEPTH, 4, CK])
    cb_in = din("cb", [128, DEPTH, 3, 4])
    wout_in = din("wout", [DEPTH, 128, 8, D])
    wup_in = din("wup", [DEPTH, NJ, 128, 2048])
    wdn_in = din("wdn", [DEPTH, NJ, 128, D])
    fw_in = din("fw", [128, DEPTH, NJ, 3])
    y_t = nc.dram_tensor("y", [NTOK, D], F32, kind="ExternalOutput")

    xres_t = dscr("xres", [NTOK, D], F32)
    qkT_t = dscr("qkT", [8, 128, NTOK], BF16)
    v_t = dscr("vS", [NTOK, 512], BF16)
    cTs_t = dscr("cTs", [4, 128, NTOK], BF16)
    attT_t = dscr("attT", [4, 128, NTOK], BF16)
    gscr_t = dscr("gscr", [8, 1536], F32)
    escr_t = dscr("escr", [4, 3, 128, 512], BF16)
    gate_t = dscr("gates", [DEPTH, 2, NSEQ, D], F32)
    wS_t = dscr("wS", [DEPTH, NJ, 128, 2560], BF16)
    wS2_t = dscr("wS2", [DEPTH, NJ, 128, 512], BF16)

    xin, y = xin_t.ap(), y_t.ap()
    xres, qkT, vS, cTs, attT = xres_t.ap(), qkT_t.ap(), v_t.ap(), cTs_t.ap(), attT_t.ap()
    gscr, escr, gates, wS, wS2 = gscr_t.ap(), escr_t.ap(), gate_t.ap(), wS_t.ap(), wS2_t.ap()
    wada, win, wout, wup, wdn = wada_in.ap(), win_in.ap(), wout_in.ap(), wup_in.ap(), wdn_in.ap()

    with ExitStack() as es:
        k = K(nc, es)
        PE, ACT, DVE, POOL, SP = k.PE, k.ACT, k.DVE, k.POOL, k.SP

        _cnt = [0]

        def sb(st, name, shape, dt):
            _cnt[0] += 1
            return st.enter_context(nc.sbuf_tensor(f"sb{_cnt[0]}_{name}", list(shape), dt))

        PS = [es.enter_context(nc.psum_tensor(f"ps{b}", [128, 512], F32)) for b in range(8)]
        PSD = [Dep() for _ in range(8)]

        def psb(b):
            return PS[b][:, :].bitcast(BF16)

        ident = sb(es, "ident", [128, 128], BF16)
        Jm = sb(es, "Jm", [128, 128], F32)
        ones_bf = sb(es, "ones_bf", [128, 128], BF16)
        blk1 = sb(es, "blk1", [128, 128], BF16)
        onesd = sb(es, "onesd", [128, 128], F32)
        scT = sb(es, "scT", [128, 8, NSEQ], F32)
        ngc = sb(es, "ngc", [128, DEPTH, 2, 8], F32)
        badac = sb(es, "badac", [128, DEPTH, 48], F32)
        qkg = sb(es, "qkg", [128, DEPTH, 2], F32)
        cwc = sb(es, "cwc", [128, DEPTH, 4, CK], F32)
        cbc = sb(es, "cbc", [128, DEPTH, 3, 4], F32)
        fwc = sb(es, "fwc", [128, DEPTH, NJ, 3], F32)
        junkA = sb(es, "junkA", [128, 1024], BF16)
        lnb_qk = sb(es, "lnb_qk", [128, 2], F32)
        d_const = Dep()
        ds_misc = k.dsem("misc")
        ds_pre = [k.dsem(f"pre{l_}") for l_ in range(DEPTH)]

        d_wS = [Dep() for _ in range(DEPTH)]

        def precast(l):
            for j in range(NJ):
                k.dma(POOL, wS[l, j, :, 0:2048], wup[l, j, :, :], [], [d_wS[l]], ds_pre[l])
                k.dma(POOL, wS[l, j, :, 2048:2560], wdn[l, j, :, 0:512], [], [d_wS[l]], ds_pre[l])
                k.dma(POOL, wS2[l, j, :, :], wdn[l, j, :, 512:1024], [], [d_wS[l]], ds_pre[l])

        k.op(POOL, lambda e: e.memset(ident[:], 0.0), [], [d_const])
        k.op(POOL, lambda e: e.affine_select(out=ident[:], in_=ident[:], pattern=[[-1, 128]],
                                             compare_op=ALU.not_equal, fill=1.0, base=0,
                                             channel_multiplier=1), [d_const], [d_const])
        k.op(POOL, lambda e: e.memset(Jm[:], 0.0), [], [d_const])
        k.op(POOL, lambda e: e.affine_select(out=Jm[:], in_=Jm[:], pattern=[[1, 128]],
                                             compare_op=ALU.not_equal, fill=1.0, base=-127,
                                             channel_multiplier=1), [d_const], [d_const])
        k.op(DVE, lambda e: e.memset(ones_bf[:], 1.0), [], [d_const])
        k.op(DVE, lambda e: e.memset(lnb_qk[:, 0:1], 64.0 * EPS), [], [d_const])
        k.op(DVE, lambda e: e.memset(lnb_qk[:, 1:2], EPS), [], [d_const])
        k.op(DVE, lambda e: e.memset(blk1[:], 0.0), [], [d_const])
        k.op(DVE, lambda e: e.memset(blk1[0:64, 0:64], 1.0), [], [d_const])
        k.op(DVE, lambda e: e.memset(blk1[64:128, 64:128], 1.0), [], [d_const])
        k.op(DVE, lambda e: e.memset(onesd[:], 1.0 / 512.0), [], [d_const])
        for (dst, src) in ((scT, cT_in), (ngc, ng_in), (badac, badac_in), (qkg, qkg_in), (cwc, cw_in),
                           (cbc, cb_in), (fwc, fw_in)):
            k.dma(SP, dst[:], src.ap(), [], [d_const], ds_misc)
        k.op(ACT, lambda e: e.activation(out=scT[:], in_=scT[:], func=AF.Silu), [d_const], [d_const])
        scTb = sb(es, "scTb", [128, 8, NSEQ], BF16)
        k.op(DVE, lambda e: e.tensor_copy(out=scTb[:], in_=scT[:]), [d_const], [d_const])

        with ExitStack() as st:
            relb = sb(st, "relb", [32, 8], F32)
            ohs = sb(st, "ohs", [32, 1536], F32)
            vms = sb(st, "vms", [8, 1536], F32)
            gv = sb(st, "gv", [8, 1536], F32)
            hk = sb(st, "hk", [128, 512], F32)
            et = sb(st, "et", [128, 512], BF16)
            d_t = Dep()
            d_gv = Dep()
            d_hk, d_et = Dep(), Dep()
            ds_e1, ds_e2, ds_e3 = k.dsem("e1"), k.dsem("e2"), k.dsem("e3")
            k.dma(SP, relb[:], relb_in.ap(), [], [d_t], ds_e1)
            k.dma(SP, ohs[:], oh_in.ap(), [], [d_t], ds_e1)
            k.dma(SP, vms[:], vm_in.ap(), [], [d_t], ds_e1)
            for c3 in range(3):
                k.op(PE, lambda e: e.matmul(PS[c3][0:8, :], lhsT=relb[:, :], rhs=ohs[:, c3 * 512:(c3 + 1) * 512],
                                            start=True, stop=True), [d_t], [PSD[c3]])
                k.op(ACT, lambda e: e.activation(out=gv[:, c3 * 512:(c3 + 1) * 512], in_=PS[c3][0:8, :],
                                                 func=AF.Exp), [PSD[c3]], [d_gv])
            k.op(DVE, lambda e: e.tensor_tensor(out=gv[:], in0=gv[:], in1=vms[:], op=ALU.mult), [d_gv, d_t], [d_gv])
            d_gscr = Dep()
            k.dma(SP, gscr[:, :], gv[:], [d_gv], [d_gscr], ds_e1)
            d_escr = k.ddep("escr", 0)
            for a in range(4):
                for pi in range(3):
                    for hh in range(2):
                        for blk in range(2):
                            off = (2 * a + hh) * 1536 + (pi * 2 + blk) * 256
                            src = bass.AP(gscr_t, off, [[1, 128], [1, 128]])
                            sl = (hh * 2 + blk) * 128
                            k.dma(SP, hk[:, sl:sl + 128], src, [d_gscr], [d_hk], ds_e2)
                    k.op(PE, lambda e: e.matmul(PS[3][:, :], lhsT=Jm[:, :], rhs=hk[:, :], start=True, stop=True),
                         [d_hk, d_const], [PSD[3]])
                    k.op(DVE, lambda e: e.tensor_copy(out=et[:], in_=PS[3][:, :]), [PSD[3]], [d_et])
                    k.dma(SP, escr[a, pi, :, :], et[:], [d_et], [d_escr], ds_e3)
            k.barrier()

        def _chk(name):
            if stop == name:
                k.barrier()
                k.dead = True

        try:
          _chk('etab')
          for l in range(depth):
            xsrc = xin if l == 0 else xres
            xsrc_name = "xin" if l == 0 else "xres"
            xdst = y if l == depth - 1 else xres
            xdst_name = "y" if l == depth - 1 else "xres"
            with ExitStack() as ls:
                MODC = sb(ls, f"MODC{l}", [128, 4, 8, NSEQ], F32)
                d_modc = Dep()
                with ExitStack() as st:
                    screp = sb(st, "screp", [128, 8, NSEQ, 128], BF16)
                    WP = [sb(st, f"wp{i}", [128, 8, 512], BF16) for i in range(2)]
                    d_wp = [Dep(), Dep()]
                    ds_wp = [k.dsem(f"wp_{i}") for i in range(2)]
                    brow = sb(st, "brow", [128, 2, D], F32)
                    gst = [sb(st, f"gst{i}", [128, 512], F32) for i in range(2)]
                    d_gst = [Dep(), Dep()]
                    ds_gst = [k.dsem(f"gst_{i}") for i in range(2)]
                    d_screp, d_brow = Dep(), Dep()
                    k.op(DVE, lambda e: e.memset(screp[:], 1.0), [], [d_screp])
                    for kc in range(8):
                        for s in range(NSEQ):
                            k.op(DVE, lambda e: e.tensor_scalar(out=screp[:, kc, s, :], in0=screp[:, kc, s, :],
                                                                scalar1=scT[:, kc, s:s + 1], scalar2=None,
                                                                op0=ALU.mult), [d_screp, d_const], [d_screp])
                    for g in range(2):
                        f0 = 2048 + g * 3072
                        k.dma(SP, brow[:, g, :], bada_in.ap()[l:l + 1, f0:f0 + D].broadcast_to([128, D]),
                              [], [d_brow], ds_misc)
                    pieces = []
                    for (kind, fbase) in ((0, 0), (1, 1024), (2, 3072), (3, 4096)):
                        for hf in range(2):
                            pieces.append(("col", kind, hf, fbase + hf * 512))
                    for g in range(2):
                        for hf in range(2):
                            pieces.append(("gate", g, hf, 2048 + g * 3072 + hf * 512))
                    d_gate = k.ddep("gates", l)
                    gi = 0
                    for pi_, (typ, kind, hf, f0) in enumerate(pieces):
                        sl = pi_ % 2
                        k.dma(POOL, WP[sl][:], wada[l, f0 // 512, :, :, :], [], [d_wp[sl]], ds_wp[sl])
                        if typ == "col":
                            bank = 4 + (pi_ % 2)
                            for fc in range(4):
                                for kc in range(8):
                                    k.op(PE, lambda e: e.matmul(PS[bank][:, fc * 4:fc * 4 + NSEQ],
                                                                lhsT=WP[sl][:, kc, fc * 128:(fc + 1) * 128],
                                                                rhs=scTb[:, kc, :], start=(kc == 0), stop=(kc == 7)),
                                         [d_wp[sl], d_const], [PSD[bank]], signal=(kc == 7))
                            fc0 = f0 // 128
                            for s in range(NSEQ):
                                k.op(DVE, lambda e: e.tensor_tensor(out=MODC[:, kind, hf * 4:hf * 4 + 4, s],
                                                                    in0=PS[bank][:, s:16:4],
                                                                    in1=badac[:, l, fc0:fc0 + 4], op=ALU.add),
                                     [PSD[bank], d_const], [d_modc])
                        else:
                            for s in range(NSEQ):
                                bank = 6 + (gi % 2)
                                for kc in range(8):
                                    k.op(PE, lambda e: e.matmul(PS[bank][:, :], lhsT=screp[:, kc, s, :],
                                                                rhs=WP[sl][:, kc, :], start=(kc == 0), stop=(kc == 7)),
                                         [d_wp[sl], d_screp], [PSD[bank]], signal=(kc == 7))
                                gs = gi % 2
                                k.op(DVE, lambda e: e.tensor_tensor(out=gst[gs][:], in0=PS[bank][:, :],
                                                                    in1=brow[:, kind, hf * 512:(hf + 1) * 512],
                                                                    op=ALU.add),
                                     [PSD[bank], d_brow], [d_gst[gs]])
                                k.dma(SP, gates[l, kind, s:s + 1, hf * 512:(hf + 1) * 512], gst[gs][0:1, :],
                                      [d_gst[gs]], [d_gate], ds_gst[gs])
                                gi += 1
                    for (kind, which) in ((1, 0), (3, 1)):
                        for s in range(NSEQ):
                            k.op(DVE, lambda e: e.tensor_scalar(out=MODC[:, kind, :, s], in0=MODC[:, kind, :, s],
                                                                scalar1=1.0, scalar2=None, op0=ALU.add),
                                 [d_modc], [d_modc])
                            k.op(DVE, lambda e: e.tensor_tensor(out=MODC[:, kind, :, s], in0=MODC[:, kind, :, s],
                                                                in1=ngc[:, l, which, :], op=ALU.mult),
                                 [d_modc, d_const], [d_modc])
                    k.barrier()
                    _chk('adaln')

                with ExitStack() as As:
                    WIN = sb(As, "WIN", [128, 8, DIN], BF16)
                    U = sb(As, "U", [128, 4, SMAX + 30], BF16)
                    DG = sb(As, "DG", [128, 4, CK, 128], BF16)
                    d_win, d_dg = Dep(), Dep()
                    ds_win = k.dsem("win")
                    for kc in range(8):
                        k.dma(POOL, WIN[:, kc, :], win[l, :, kc, :], [], [d_win], ds_win)
                    for c in range(4):
                        for j in range(CK):
                            k.op(POOL, lambda e: e.tensor_scalar(out=DG[:, c, j, :], in0=ident[:],
                                                                 scalar1=cwc[:, l, c, j:j + 1], scalar2=None,
                                                                 op0=ALU.mult), [d_const], [d_dg])
                    if l == 0:
                        precast(0)
                    for s in range(NSEQ):
                        S = seqs[s]
                        T0 = seq_base[s]
                        NT = S // 512
                        d_U = [Dep() for _ in range(NT)]
                        d_Uh = Dep()
                        with ExitStack() as st:
                            XA = [sb(st, f"XA{i}", [128, 4, D], F32) for i in range(2)]
                            d_xa = [Dep(), Dep()]
                            ds_xa = [k.dsem(f"xa_{i}") for i in range(2)]
                            ybf = sb(st, "ybf", [128, 4, D], BF16)
                            d_ybf = Dep()
                            hT = [sb(st, f"hT{i}", [128, 8, 512], BF16) for i in range(2)]
                            d_hT = [Dep(), Dep()]
                            ssq = sb(st, "ssq", [128, 4], F32)
                            rstd = sb(st, "rstd", [128, 4], F32)
                            d_ssq, d_rstd = Dep(), Dep()
                            QKst = [sb(st, f"QKst{i}", [128, 8, 512], BF16) for i in range(2)]
                            d_qkst = [Dep(), Dep()]
                            ds_qk = [k.dsem(f"qk_{i}") for i in range(2)]
                            Vst = [sb(st, f"Vst{i}", [128, 4, 512], BF16) for i in range(2)]
                            d_vst = [Dep(), Dep()]
                            ds_v = [k.dsem(f"v_{i}") for i in range(2)]
                            sqb = [sb(st, f"sqb{i}", [128, 512], BF16) for i in range(2)]
                            d_sqb = [Dep(), Dep()]
                            rt = [sb(st, f"rt{i}", [128, 512], F32) for i in range(2)]
                            d_rt = [Dep(), Dep()]
                            sg = [sb(st, f"sg{i}", [128, 512], F32) for i in range(2)]
                            d_sg = [Dep(), Dep()]
                            k.op(POOL, lambda e: e.memset(U[:, :, 0:15], 0.0), [], [d_Uh])
                            k.op(POOL, lambda e: e.memset(U[:, :, 15 + S:30 + S], 0.0), [], [d_Uh])

                            def load_x(i):
                                t0 = T0 + i * 512
                                k.dma(SP, XA[i % 2][:], xsrc[t0:t0 + 512, :].rearrange("(s p) f -> p s f", p=128),
                                      [k.ddep(xsrc_name, t0 // 512)], [d_xa[i % 2]], ds_xa[i % 2])

                            rot = {"p": 0, "s": 0}

                            def fe_elem(i):
                                X = XA[i % 2]
                                dX = d_xa[i % 2]
                                k.op(DVE, lambda e: e.memset(ssq[:], 0.0), [], [d_ssq])
                                for sub in range(4):
                                    k.op(ACT, lambda e: e.activation(out=junkA[:], in_=X[:, sub, :], func=AF.Square,
                                                                     accum_out=ssq[:, sub:sub + 1]),
                                         [dX, d_ssq], [d_ssq])
                                k.op(ACT, lambda e: e.activation(out=rstd[:], in_=ssq[:], func=AF.Sqrt,
                                                                 scale=1.0 / D, bias=EPS), [d_ssq], [d_rstd])
                                k.op(DVE, lambda e: e.reciprocal(out=rstd[:], in_=rstd[:]), [d_rstd], [d_rstd])
                                for sub in range(4):
                                    if sub % 2 == 0:
                                        k.op(ACT, lambda e: e.activation(out=ybf[:, sub, :], in_=X[:, sub, :],
                                                                         func=AF.Copy, scale=rstd[:, sub:sub + 1]),
                                             [dX, d_rstd], [d_ybf])
                                    else:
                                        k.op(DVE, lambda e: e.tensor_scalar(out=ybf[:, sub, :], in0=X[:, sub, :],
                                                                            scalar1=rstd[:, sub:sub + 1], scalar2=None,
                                                                            op0=ALU.mult), [dX, d_rstd], [d_ybf])

                            def fe_pe(i):
                                H = hT[i % 2]
                                dH = d_hT[i % 2]
                                for kp in range(4):
                                    bank = kp % 2
                                    for k2 in range(2):
                                        kc = kp * 2 + k2
                                        for sub in range(4):
                                            o = (k2 * 4 + sub) * 128
                                            k.op(PE, lambda e: e.transpose(psb(bank)[:, o:o + 128],
                                                                           ybf[:, sub, kc * 128:(kc + 1) * 128],
                                                                           ident[:]),
                                                 [d_ybf, d_const], [PSD[bank]], signal=(k2 == 1 and sub == 3))
                                    for k2 in range(2):
                                        kc = kp * 2 + k2
                                        if k2 == 0:
                                            k.op(DVE, lambda e: e.tensor_scalar(out=H[:, kc, :], in0=psb(bank)[:, 0:512],
                                                                                scalar1=MODC[:, 1, kc, s:s + 1],
                                                                                scalar2=MODC[:, 0, kc, s:s + 1],
                                                                                op0=ALU.mult, op1=ALU.add),
                                                 [PSD[bank], d_modc], [dH])
                                        else:
                                            k.op(ACT, lambda e: e.activation(out=H[:, kc, :], in_=psb(bank)[:, 512:1024],
                                                                             func=AF.Identity, scale=MODC[:, 1, kc, s:s + 1],
                                                                             bias=MODC[:, 0, kc, s:s + 1]),
                                                 [PSD[bank], d_modc], [dH])

                            def proj_qk(i):
                                t0 = T0 + i * 512
                                H = hT[i % 2]
                                dH = d_hT[i % 2]
                                QK = QKst[i % 2]
                                dQK = d_qkst[i % 2]
                                banks = {}

                                def qk_norm(j):
                                    bank = banks[j]
                                    sq_ = sqb[j % 2]
                                    sbank = 6 + rot["s"] % 2
                                    rot["s"] += 1
                                    k.op(PE, lambda e: e.matmul(PS[sbank][:, :], lhsT=blk1[:, :], rhs=sq_[:],
                                                                start=True, stop=True),
                                         [d_sqb[j % 2], d_const], [PSD[sbank]])
                                    r_ = rt[j % 2]
                                    k.op(ACT, lambda e: e.activation(out=r_[:], in_=PS[sbank][:, :], func=AF.Ln,
                                                                     bias=lnb_qk[:, 0:1], scale=1.0),
                                         [PSD[sbank], d_const], [d_rt[j % 2]])
                                    k.op(ACT, lambda e: e.activation(out=r_[:], in_=r_[:], func=AF.Exp, scale=-0.5),
                                         [d_rt[j % 2]], [d_rt[j % 2]])
                                    k.op(DVE, lambda e: e.scalar_tensor_tensor(out=QK[:, j, :], in0=PS[bank][:, :],
                                                                               scalar=qkg[:, l, (j // 4):(j // 4) + 1],
                                                                               in1=r_[:], op0=ALU.mult, op1=ALU.mult),
                                         [PSD[bank], d_rt[j % 2], d_const], [dQK])

                                for j in range(8):
                                    bank = 2 + rot["p"] % 4
                                    rot["p"] += 1
                                    banks[j] = bank
                                    for kc in range(8):
                                        k.op(PE, lambda e: e.matmul(PS[bank][:, :], lhsT=WIN[:, kc, j * 128:(j + 1) * 128],
                                                                    rhs=H[:, kc, :], start=(kc == 0), stop=(kc == 7)),
                                             [d_win, dH], [PSD[bank]], signal=(kc == 7))
                                    sq_ = sqb[j % 2]
                                    k.op(ACT, lambda e: e.activation(out=sq_[:], in_=PS[bank][:, :], func=AF.Square),
                                         [PSD[bank]], [d_sqb[j % 2]])
                                    if j >= 1:
                                        qk_norm(j - 1)
                                qk_norm(7)
                                k.dma(SP, qkT[:, :, t0:t0 + 512].rearrange("j p t -> p j t"), QK[:],
                                      [dQK], [k.ddep("qkT", t0 // 512)], ds_qk[i % 2])

                            def proj_rest(i):
                                t0 = T0 + i * 512
                                H = hT[i % 2]
                                dH = d_hT[i % 2]
                                Vs = Vst[i % 2]
                                dV = d_vst[i % 2]
                                for sub in range(4):
                                    bank = 2 + rot["p"] % 4
                                    rot["p"] += 1
                                    for kc in range(8):
                                        k.op(PE, lambda e: e.matmul(PS[bank][:, :], lhsT=H[:, kc, sub * 128:(sub + 1) * 128],
                                                                    rhs=WIN[:, kc, 1024:1536], start=(kc == 0), stop=(kc == 7)),
                                             [d_win, dH], [PSD[bank]], signal=(kc == 7))
                                    k.op(ACT, lambda e: e.activation(out=Vs[:, sub, :], in_=PS[bank][:, :], func=AF.Copy),
                                         [PSD[bank]], [dV])
                                k.dma(SP, vS[t0:t0 + 512, :].rearrange("(s p) f -> p s f", p=128), Vs[:],
                                      [dV], [k.ddep("vS", t0 // 512)], ds_v[i % 2])
                                for c in range(4):
                                    bv = 2 + rot["p"] % 4
                                    rot["p"] += 1
                                    bg = 2 + rot["p"] % 4
                                    rot["p"] += 1
                                    for kc in range(8):
                                        k.op(PE, lambda e: e.matmul(PS[bv][:, :], lhsT=WIN[:, kc, 1536 + c * 128:1536 + (c + 1) * 128],
                                                                    rhs=H[:, kc, :], start=(kc == 0), stop=(kc == 7)),
                                             [d_win, dH], [PSD[bv]], signal=(kc == 7))
                                    for kc in range(8):
                                        k.op(PE, lambda e: e.matmul(PS[bg][:, :], lhsT=WIN[:, kc, 2048 + c * 128:2048 + (c + 1) * 128],
                                                                    rhs=H[:, kc, :], start=(kc == 0), stop=(kc == 7)),
                                             [d_win, dH], [PSD[bg]], signal=(kc == 7))
                                    s_ = sg[c % 2]
                                    k.op(ACT, lambda e: e.activation(out=s_[:], in_=PS[bg][:, :], func=AF.Sigmoid),
                                         [PSD[bg]], [d_sg[c % 2]])
                                    k.op(DVE, lambda e: e.tensor_tensor(out=U[:, c, 15 + i * 512:15 + (i + 1) * 512],
                                                                        in0=PS[bv][:, :], in1=s_[:], op=ALU.mult),
                                         [PSD[bv], d_sg[c % 2]], [d_U[i]])

                            load_x(0)
                            if NT > 1:
                                load_x(1)
                            fe_elem(0)
                            fe_pe(0)
                            for i in range(NT):
                                if i + 1 < NT:
                                    fe_elem(i + 1)
                                    if i + 2 < NT:
                                        load_x(i + 2)
                                proj_qk(i)
                                if i + 1 < NT:
                                    fe_pe(i + 1)
                                proj_rest(i)
                            k.barrier()
                            _chk('A')
                        with ExitStack() as st:
                            cp = sb(st, "cp", [128, 4, 512], F32)
                            sq4 = sb(st, "sq4", [128, 4, 512], F32)
                            m2 = sb(st, "m2", [128, 512], F32)
                            var = sb(st, "var", [128, 512], F32)
                            cst = [sb(st, f"cst{i}", [128, 4, 512], BF16) for i in range(2)]
                            d_cp, d_sq4, d_m2, d_var = Dep(), Dep(), Dep(), Dep()
                            d_cst = [Dep(), Dep()]
                            ds_c = [k.dsem(f"c_{i}") for i in range(2)]
                            for i in range(NT):
                                t0 = T0 + i * 512
                                ureads = [d_Uh] + [d_U[x] for x in (i - 1, i, i + 1) if 0 <= x < NT]
                                for c in range(4):
                                    bank = 2 + c
                                    for j in range(CK):
                                        k.op(PE, lambda e: e.matmul(PS[bank][:, :], lhsT=DG[:, c, j, :],
                                                                    rhs=U[:, c, i * 512 + j:i * 512 + j + 512],
                                                                    start=(j == 0), stop=(j == CK - 1)),
                                             [d_dg] + ureads, [PSD[bank]], signal=(j == CK - 1))
                                    k.op(ACT, lambda e: e.activation(out=cp[:, c, :], in_=PS[bank][:, :], func=AF.Identity,
                                                                     bias=cbc[:, l, 0, c:c + 1], scale=1.0),
                                         [PSD[bank], d_const], [d_cp])
                                    k.op(ACT, lambda e: e.activation(out=sq4[:, c, :], in_=PS[bank][:, :], func=AF.Square,
                                                                     bias=cbc[:, l, 0, c:c + 1], scale=1.0),
                                         [PSD[bank], d_const], [d_sq4])
                                for c in range(4):
                                    k.op(PE, lambda e: e.matmul(PS[6][:, :], lhsT=onesd[:, :], rhs=cp[:, c, :],
                                                                start=(c == 0), stop=(c == 3)),
                                         [d_cp, d_const], [PSD[6]], signal=(c == 3))
                                for c in range(4):
                                    k.op(PE, lambda e: e.matmul(PS[7][:, :], lhsT=onesd[:, :], rhs=sq4[:, c, :],
                                                                start=(c == 0), stop=(c == 3)),
                                         [d_sq4, d_const], [PSD[7]], signal=(c == 3))
                                k.op(ACT, lambda e: e.activation(out=m2[:], in_=PS[6][:, :], func=AF.Square),
                                     [PSD[6]], [d_m2])
                                k.op(DVE, lambda e: e.tensor_tensor(out=var[:], in0=PS[7][:, :], in1=m2[:], op=ALU.subtract),
                                     [PSD[7], d_m2], [d_var])
                                k.op(ACT, lambda e: e.activation(out=var[:], in_=var[:], func=AF.Ln, bias=lnb_qk[:, 1:2], scale=1.0),
                                     [d_var, d_const], [d_var])
                                k.op(ACT, lambda e: e.activation(out=var[:], in_=var[:], func=AF.Exp, scale=-0.5),
                                     [d_var], [d_var])
                                C = cst[i % 2]
                                for c in range(4):
                                    k.op(DVE, lambda e: e.tensor_tensor(out=cp[:, c, :], in0=cp[:, c, :], in1=PS[6][:, :],
                                                                        op=ALU.subtract), [d_cp, PSD[6]], [d_cp])
                                    k.op(DVE, lambda e: e.tensor_tensor(out=cp[:, c, :], in0=cp[:, c, :], in1=var[:],
                                                                        op=ALU.mult), [d_cp, d_var], [d_cp])
                                    k.op(ACT, lambda e: e.activation(out=C[:, c, :], in_=cp[:, c, :], func=AF.Silu,
                                                                     scale=cbc[:, l, 1, c:c + 1], bias=cbc[:, l, 2, c:c + 1]),
                                         [d_cp, d_const], [d_cst[i % 2]])
                                k.dma(SP, cTs[:, :, t0:t0 + 512].rearrange("c p t -> p c t"), C[:],
                                      [d_cst[i % 2]], [k.ddep("cTs", t0 // 512)], ds_c[i % 2])
                            k.barrier()
                            _chk('A2')

                with ExitStack() as Bs:
                    WOUT = sb(Bs, "WOUT", [128, 8, D], BF16)
                    d_wout = Dep()
                    ds_wout = k.dsem("wout")
                    for kc in range(0, 8, 4):
                        k.dma(POOL, WOUT[:, kc:kc + 4, :], wout[l, :, kc:kc + 4, :], [], [d_wout], ds_wout)
                    if l + 1 < depth:
                        precast(l + 1)
                    with ExitStack() as st:
                        QT = [sb(st, f"QT{i}", [128, 2, SMAX], BF16) for i in range(2)]
                        KT = [sb(st, f"KT{i}", [128, SMAX], BF16) for i in range(2)]
                        ET = [sb(st, f"ET{i}", [128, 3, 2, 2, 128], BF16) for i in range(2)]
                        d_qt = [Dep(), Dep()]
                        ds_qt = [k.dsem(f"qt_{i}") for i in range(2)]
                        acc = sb(st, "acc", [128, 2, SMAX], F32)
                        d_acc = Dep()
                        VB = [sb(st, f"VB{i}", [128, 9, 512], BF16) for i in range(2)]
                        d_vb = [Dep(), Dep()]
                        ds_vb = [k.dsem(f"vb_{i}") for i in range(2)]
                        ex = [sb(st, f"ex{i}", [128, 2, 2, 128], BF16) for i in range(2)]
                        d_ex = [Dep(), Dep()]
                        PT = [sb(st, f"PT{i}", [128, 2, 2, 128], BF16) for i in range(2)]
                        d_pt = [Dep(), Dep()]
                        ast = [sb(st, f"ast{i}", [128, SMAX], BF16) for i in range(1)] * 2
                        d_ast = [Dep()] * 2
                        ds_ast = [k.dsem("ast_0")] * 2
                        pairs = [(s, a) for s in range(NSEQ) for a in range(4)]
                        QTd = {d_: sb(st, f"QTd{d_}", [128, 2, SMAX], BF16) for d_ in (4, 16)}
                        KTd = {d_: sb(st, f"KTd{d_}", [128, SMAX], BF16) for d_ in (4, 16)}
                        d_qtd = {4: Dep(), 16: Dep()}
                        for i2 in range(2):
                            k.op(DVE, lambda e: e.memset(QT[i2][64:128, 0, :], 0.0), [], [d_qt[i2]])
                            k.op(DVE, lambda e: e.memset(QT[i2][0:64, 1, :], 0.0), [], [d_qt[i2]])

                        def load_pair(idx):
                            s, a = pairs[idx]
                            S, T0 = seqs[s], seq_base[s]
                            sl = idx % 2
                            rds = [k.ddep("qkT", t) for t in range(T0 // 512, (T0 + S) // 512)]
                            k.dma(SP, QT[sl][0:64, 0, 0:S], qkT[a, 0:64, T0:T0 + S], rds, [d_qt[sl]], ds_qt[sl])
                            k.dma(SP, QT[sl][64:128, 1, 0:S], qkT[a, 64:128, T0:T0 + S], rds, [d_qt[sl]], ds_qt[sl])
                            k.dma(SP, KT[sl][:, 0:S], qkT[4 + a, :, T0:T0 + S], rds, [d_qt[sl]], ds_qt[sl])
                            k.dma(SP, ET[sl][:].rearrange("p a h b c -> p a (h b c)"),
                                  escr[a, :, :, :].rearrange("a p x -> p a x"),
                                  [k.ddep("escr", 0)], [d_qt[sl]], ds_qt[sl])

                        def segments(S):
                            out = []
                            for pi, (_, d) in enumerate(PATTERNS):
                                L = S // d
                                NB = L // 128
                                for c in range(d):
                                    for m0 in range(0, NB + 1, 8):
                                        m1 = min(m0 + 8, NB + 1)
                                        b0 = max(m0 - 1, 0)
                                        b1 = min(m1 - 1, NB - 1)
                                        out.append((pi, d, c, m0, m1, b0, b1 - b0 + 1, NB))
                            return out

                        LOOK = 5
                        NSB = LOOK + 1
                        exs = ex + [sb(st, f"exx{i}", [128, 2, 2, 128], BF16) for i in range(NSB - 2)]
                        d_exs = d_ex + [Dep() for _ in range(NSB - 2)]
                        PTs = PT + [sb(st, f"PTx{i}", [128, 2, 2, 128], BF16) for i in range(NSB - 2)]
                        d_pts = d_pt + [Dep() for _ in range(NSB - 2)]
                        cnt = {"seg": 0, "g": 0}
                        load_pair(0)
                        for idx, (s, a) in enumerate(pairs):
                            S, T0 = seqs[s], seq_base[s]
                            sl = idx % 2
                            if idx + 1 < len(pairs):
                                load_pair(idx + 1)
                            segs = segments(S)
                            seg_slot = {}

                            def load_v(si):
                                (pi, d, c, m0, m1, b0, nb, NB) = segs[si]
                                slot = cnt["seg"] % 2
                                cnt["seg"] += 1
                                seg_slot[si] = slot
                                r0 = T0 + c + d * 128 * b0
                                r1 = r0 + d * 128 * nb
                                rds = [k.ddep("vS", t) for t in range(r0 // 512, min((r1 - 1) // 512 + 1, (T0 + S) // 512))]
                                src = vS[r0:r1 - d + 1:d, :] if d > 1 else vS[r0:r1, :]
                                k.dma(SP, VB[slot][:, 0:nb, :], src.rearrange("(b p) f -> p b f", p=128),
                                      rds, [d_vb[slot]], ds_vb[slot])

                            groups = []
                            for si, (pi, d, c, m0, m1, b0, nb, NB) in enumerate(segs):
                                for m in range(m0, m1):
                                    groups.append((si, m == m0, pi, d, c, m, b0, NB))
                            Q_, K_, E_ = QT[sl], KT[sl], ET[sl]
                            ginfo = {}

                            def geom(gi):
                                (si, first, pi, d, c, m, b0, NB) = groups[gi]
                                blks = []
                                if m >= 1:
                                    blks.append((0, m - 1))
                                if m <= NB - 1:
                                    blks.append((1, m))
                                qlo = 64 if m == 0 else 0
                                qhi = 64 if m == NB else 128
                                nq = qhi - qlo
                                tq0 = c + d * (128 * m - 64 + qlo)
                                tq1 = tq0 + d * (nq - 1) + 1
                                return blks, qlo, qhi, tq0, tq1

                            def emit_qk(gi):
                                (si, first, pi, d, c, m, b0, NB) = groups[gi]
                                blks, qlo, qhi, tq0, tq1 = geom(gi)
                                gq = cnt["g"] % NSB
                                cnt["g"] += 1
                                ginfo[gi] = gq
                                sbank = gq
                                SP4 = PS[sbank][:, :].rearrange("p (h b c) -> p h b c", h=2, b=2)
                                L_ = S // d
                                for hh in range(2):
                                    for (blk, bidx) in blks:
                                        if d == 1:
                                            tk0 = 128 * bidx
                                            lhs_ap = K_[:, tk0:tk0 + 128]
                                            rhs_ap = Q_[:, hh, tq0:tq1]
                                            rds = [d_qt[sl]]
                                        else:
                                            kb = c * L_ + 128 * bidx
                                            qb = c * L_ + 128 * m - 64 + qlo
                                            lhs_ap = KTd[d][:, kb:kb + 128]
                                            rhs_ap = QTd[d][:, hh, qb:qb + (qhi - qlo)]
                                            rds = [d_qtd[d]]
                                        k.op(PE, lambda e: e.matmul(SP4[:, hh, blk, qlo:qhi], lhsT=lhs_ap, rhs=rhs_ap,
                                                                    start=True, stop=True),
                                             rds, [PSD[sbank]])
                                bsel = slice(blks[0][0], blks[-1][0] + 1)
                                EX, P_ = exs[gq], PTs[gq]
                                k.op(ACT, lambda e: e.activation(out=EX[:, :, bsel, qlo:qhi], in_=SP4[:, :, bsel, qlo:qhi],
                                                                 func=AF.Exp, scale=8.0),
                                     [PSD[sbank]], [d_exs[gq]])
                                k.op(DVE, lambda e: e.tensor_tensor(out=P_[:, :, bsel, qlo:qhi], in0=EX[:, :, bsel, qlo:qhi],
                                                                    in1=E_[:, pi, :, bsel, qlo:qhi], op=ALU.mult),
                                     [d_exs[gq], d_qt[sl]], [d_pts[gq]])

                            def emit_pv(gi):
                                (si, first, pi, d, c, m, b0, NB) = groups[gi]
                                if first and si + 1 < len(segs):
                                    load_v(si + 1)
                                blks, qlo, qhi, tq0, tq1 = geom(gi)
                                gq = ginfo.pop(gi)
                                P_ = PTs[gq]
                                vsl = seg_slot[si]
                                V, dV = VB[vsl], d_vb[vsl]
                                obank = 6 + gi % 2
                                OD = PS[obank][:, 0:256].rearrange("p (t c) -> p t c", t=2)
                                for hh in range(2):
                                    for bi, (blk, bidx) in enumerate(blks):
                                        k.op(PE, lambda e: e.matmul(OD[hh * 64:(hh + 1) * 64, 0, qlo:qhi],
                                                                    lhsT=V[:, bidx - b0, a * 128 + hh * 64:a * 128 + (hh + 1) * 64],
                                                                    rhs=P_[:, hh, blk, qlo:qhi],
                                                                    start=(bi == 0), stop=(bi == len(blks) - 1)),
                                             [dV, d_pts[gq]], [PSD[obank]], signal=False)
                                    for bi, (blk, bidx) in enumerate(blks):
                                        last = (hh == 1 and bi == len(blks) - 1)
                                        k.op(PE, lambda e: e.matmul(OD[hh * 64:(hh + 1) * 64, 1, qlo:qhi],
                                                                    lhsT=ones_bf[:, 0:64],
                                                                    rhs=P_[:, hh, blk, qlo:qhi],
                                                                    start=(bi == 0), stop=(bi == len(blks) - 1)),
                                             [d_const, d_pts[gq]], [PSD[obank]], signal=last)
                                if pi == 0:
                                    k.op(DVE, lambda e: e.tensor_copy(out=acc[:, :, tq0:tq1], in_=OD[:, :, qlo:qhi]),
                                         [PSD[obank]], [d_acc])
                                else:
                                    k.op(DVE, lambda e: e.tensor_tensor(out=acc[:, :, tq0:tq1:d], in0=OD[:, :, qlo:qhi],
                                                                        in1=acc[:, :, tq0:tq1:d], op=ALU.add),
                                         [PSD[obank], d_acc], [d_acc])

                            load_v(0)
                            NG = len(groups)
                            copies = []
                            for d_ in (4, 16):
                                for hh_ in range(2):
                                    copies.append((d_, QTd[d_][:, hh_, 0:S], Q_[:, hh_, 0:S]))
                                copies.append((d_, KTd[d_][:, 0:S], K_[:, 0:S]))

                            def emit_copy():
                                d_, dst, src = copies.pop(0)
                                k.op(POOL, lambda e: e.tensor_copy(out=dst.rearrange("p (c i) -> p c i", c=d_),
                                                                   in_=src.rearrange("p (i c) -> p c i", c=d_)),
                                     [d_qt[sl]], [d_qtd[d_]])

                            for gi in range(NG + LOOK):
                                if gi < NG:
                                    while copies:
                                        emit_copy()
                                    emit_qk(gi)
                                if gi - LOOK >= 0:
                                    emit_pv(gi - LOOK)
                            A_ = ast[idx % 2]
                            for t in range(S // 512):
                                cs = slice(t * 512, (t + 1) * 512)
                                k.op(ACT, lambda e: e.activation(out=acc[:, 1, cs], in_=acc[:, 1, cs], func=AF.Ln), [d_acc], [d_acc])
                                k.op(ACT, lambda e: e.activation(out=acc[:, 1, cs], in_=acc[:, 1, cs], func=AF.Exp, scale=-1.0),
                                     [d_acc], [d_acc])
                                k.op(DVE, lambda e: e.tensor_tensor(out=A_[:, cs], in0=acc[:, 0, cs], in1=acc[:, 1, cs],
                                                                    op=ALU.mult), [d_acc], [d_ast[idx % 2]])
                            k.dma(SP, attT[a, :, T0:T0 + S], A_[:, 0:S], [d_ast[idx % 2]],
                                  [k.ddep("attT", t) for t in range(T0 // 512, (T0 + S) // 512)], ds_ast[idx % 2])
                        k.barrier()
                        _chk('B')

                    with ExitStack() as st:
                        NTILE = NTOK // 512
                        tile_seq = []
                        for s in range(NSEQ):
                            for i in range(seqs[s] // 512):
                                tile_seq.append((s, i, seqs[s] // 512))
                        GATE = [sb(st, f"GATE{i}", [128, 2, D], F32) for i in range(2)]
                        d_gatesb = [Dep(), Dep()]
                        ds_gate = [k.dsem(f"gate_{i}") for i in range(2)]
                        RING = 4
                        WR = [sb(st, f"WR{i}", [128, 2560], BF16) for i in range(RING)]
                        d_wr = [Dep() for _ in range(RING)]
                        ds_wr = [k.dsem(f"wr_{i}") for i in range(RING)]
                        WR2 = [sb(st, f"WR2{i}", [128, 512], BF16) for i in range(RING)]
                        d_wr2 = [Dep() for _ in range(RING)]
                        ds_wr2 = [k.dsem(f"wr2_{i}") for i in range(RING)]
                        mT = sb(st, "mT", [128, NJ, 512], BF16)
                        d_mT = [Dep() for _ in range(NJ)]
                        XC = [sb(st, f"XC{i}", [128, 4, D], F32) for i in range(3)]
                        d_xc = [Dep() for _ in range(3)]
                        ds_xc = [k.dsem(f"xc_{i}") for i in range(3)]
                        ds_xo = [k.dsem(f"xo_{i}") for i in range(3)]
                        h2T = [sb(st, f"h2T{i}", [128, 8, 513], BF16) for i in range(3)]
                        d_h2 = [Dep() for _ in range(3)]
                        CAT = [sb(st, f"CAT{i}", [128, 8, 512], BF16) for i in range(2)]
                        d_cat = [Dep(), Dep()]
                        ds_cat = [k.dsem(f"cat_{i}") for i in range(2)]
                        ybf2 = sb(st, "ybf2", [128, 4, D], BF16)
                        d_ybf2 = Dep()
                        ssq2 = sb(st, "ssq2", [128, 4], F32)
                        rstd2 = sb(st, "rstd2", [128, 4], F32)
                        d_ssq2, d_rstd2 = Dep(), Dep()
                        tmpo = [sb(st, f"tmpo{i}", [128, 512], F32) for i in range(2)]
                        d_tmpo = [Dep(), Dep()]
                        gbuf = [sb(st, f"gbuf{i}", [128, 514], F32) for i in range(2)]
                        d_gbuf = [Dep(), Dep()]
                        t1 = [sb(st, f"t1{i}", [128, 512], F32) for i in range(2)]
                        d_t1 = [Dep(), Dep()]
                        ge = [sb(st, f"ge{i}", [128, 512], F32) for i in range(2)]
                        d_ge = [Dep(), Dep()]
                        Gsave = sb(st, "Gsave", [128, NJ, 2], F32)
                        d_gsave = [Dep() for _ in range(NJ)]
                        junkB = junkA
                        abuf = [sb(st, f"abuf{i}", [128, 512], F32) for i in range(3)]
                        d_abuf = [Dep() for _ in range(3)]

                        gate_loaded = {}

                        def load_gate(s):
                            if s in gate_loaded:
                                return
                            sl = s % 2
                            gate_loaded[s] = sl
                            for g in range(2):
                                k.dma(SP, GATE[sl][:, g, :], gates[l, g, s:s + 1, :].broadcast_to([128, D]),
                                      [k.ddep("gates", l)], [d_gatesb[sl]], ds_gate[sl])

                        pend_loads = []

                        def load_s1(t, deferred=False):
                            s, i, nt = tile_seq[t]
                            load_gate(s)
                            t0 = t * 512
                            pieces = []
                            for sub in range(4):
                                pieces.append(lambda sub=sub: k.dma(
                                    SP, XC[t % 3][:, sub, :], xsrc[t0 + sub * 128:t0 + (sub + 1) * 128, :],
                                    [k.ddep(xsrc_name, t)], [d_xc[t % 3]], ds_xc[t % 3]))
                            pieces.append(lambda: k.dma(SP, CAT[t % 2][:, 0:4, :],
                                                        attT[:, :, t0:t0 + 512].rearrange("a p t -> p a t"),
                                                        [k.ddep("attT", t)], [d_cat[t % 2]], ds_cat[t % 2]))
                            pieces.append(lambda: k.dma(SP, CAT[t % 2][:, 4:8, :],
                                                        cTs[:, :, t0:t0 + 512].rearrange("a p t -> p a t"),
                                                        [k.ddep("cTs", t)], [d_cat[t % 2]], ds_cat[t % 2]))
                            if deferred:
                                pend_loads.extend(pieces)
                            else:
                                for p_ in pieces:
                                    p_()

                        def flush_loads(n=None):
                            while pend_loads and (n is None or n > 0):
                                pend_loads.pop(0)()
                                if n is not None:
                                    n -= 1

                        s1rot = [0, 0]

                        def stage1a(t):
                            s, i, nt = tile_seq[t]
                            X, dX = XC[t % 3], d_xc[t % 3]
                            C_, dC = CAT[t % 2], d_cat[t % 2]
                            G_, dG = GATE[s % 2], d_gatesb[s % 2]
                            H2, dH2 = h2T[t % 3], d_h2[t % 3]
                            k.op(POOL, lambda e: e.memset(ssq2[:], 0.0), [], [d_ssq2])
                            for sub in range(4):
                                for hf in range(2):
                                    bank = 4 + s1rot[0] % 2
                                    s1rot[0] += 1
                                    for kc in range(8):
                                        k.op(PE, lambda e: e.matmul(PS[bank][:, :], lhsT=C_[:, kc, sub * 128:(sub + 1) * 128],
                                                                    rhs=WOUT[:, kc, hf * 512:(hf + 1) * 512],
                                                                    start=(kc == 0), stop=(kc == 7)),
                                             [dC, d_wout], [PSD[bank]], signal=(kc == 7))
                                    tm = tmpo[s1rot[0] % 2]
                                    dtm = d_tmpo[s1rot[0] % 2]
                                    k.op(DVE, lambda e: e.tensor_tensor(out=tm[:], in0=PS[bank][:, :],
                                                                        in1=G_[:, 0, hf * 512:(hf + 1) * 512], op=ALU.mult),
                                         [PSD[bank], dG], [dtm])
                                    k.op(POOL, lambda e: e.tensor_tensor(out=X[:, sub, hf * 512:(hf + 1) * 512],
                                                                         in0=X[:, sub, hf * 512:(hf + 1) * 512],
                                                                         in1=tm[:], op=ALU.add), [dtm, dX], [dX])
                                k.op(ACT, lambda e: e.activation(out=junkB[:], in_=X[:, sub, :], func=AF.Square,
                                                                 accum_out=ssq2[:, sub:sub + 1]), [dX, d_ssq2], [d_ssq2])
                            k.op(ACT, lambda e: e.activation(out=rstd2[:], in_=ssq2[:], func=AF.Sqrt, scale=1.0 / D, bias=EPS),
                                 [d_ssq2], [d_rstd2])
                            k.op(DVE, lambda e: e.reciprocal(out=rstd2[:], in_=rstd2[:]), [d_rstd2], [d_rstd2])
                            for sub in range(4):
                                k.op(ACT, lambda e: e.activation(out=ybf2[:, sub, :], in_=X[:, sub, :], func=AF.Copy,
                                                                 scale=rstd2[:, sub:sub + 1]), [dX, d_rstd2], [d_ybf2])

                        def stage1b(t):
                            s, i, nt = tile_seq[t]
                            H2, dH2 = h2T[t % 3], d_h2[t % 3]
                            for kp in range(4):
                                bank = 6 + kp % 2
                                for k2 in range(2):
                                    kc = kp * 2 + k2
                                    for sub in range(4):
                                        o = (k2 * 4 + sub) * 128
                                        k.op(PE, lambda e: e.transpose(psb(bank)[:, o:o + 128],
                                                                       ybf2[:, sub, kc * 128:(kc + 1) * 128], ident[:]),
                                             [d_ybf2, d_const], [PSD[bank]], signal=(k2 == 1 and sub == 3))
                                for k2 in range(2):
                                    kc = kp * 2 + k2
                                    if k2 == 0:
                                        k.op(DVE, lambda e: e.tensor_scalar(out=H2[:, kc, 0:512],
                                                                            in0=psb(bank)[:, 0:512],
                                                                            scalar1=MODC[:, 3, kc, s:s + 1],
                                                                            scalar2=MODC[:, 2, kc, s:s + 1],
                                                                            op0=ALU.mult, op1=ALU.add),
                                             [PSD[bank], d_modc], [dH2])
                                    else:
                                        k.op(ACT, lambda e: e.activation(out=H2[:, kc, 0:512], in_=psb(bank)[:, 512:1024],
                                                                         func=AF.Identity, scale=MODC[:, 3, kc, s:s + 1],
                                                                         bias=MODC[:, 2, kc, s:s + 1]),
                                             [PSD[bank], d_modc], [dH2])
                            if i > 0:
                                Hp, dHp = h2T[(t - 1) % 3], d_h2[(t - 1) % 3]
                                k.op(POOL, lambda e: e.tensor_copy(out=Hp[:, :, 512:513], in_=H2[:, :, 0:1]), [dH2], [dHp])
                            if i == nt - 1:
                                k.op(POOL, lambda e: e.memset(H2[:, :, 512:513], 0.0), [], [dH2])

                        wcount = [0, 0]

                        def load_w(j):
                            r = wcount[0] % RING
                            wcount[0] += 1
                            k.dma(SP, WR[r][:], wS[l, j, :, :], [d_wS[l]], [d_wr[r]], ds_wr[r])
                            return r

                        def load_w2(j):
                            r = wcount[1] % RING
                            wcount[1] += 1
                            k.dma(SP, WR2[r][:], wS2[l, j, :, :], [d_wS[l]], [d_wr2[r]], ds_wr2[r])
                            return r

                        pre_w, pre_w2 = {}, {}

                        def prefetch_w(t):
                            pre_w[t] = {j: load_w(j) for j in range(min(RING, NJ))}

                        def prefetch_w2(t):
                            pre_w2[t] = {j: load_w2(j) for j in range(min(RING - 1, NJ))}

                        def stage2a(t):
                            s, i, nt = tile_seq[t]
                            X, dX = XC[t % 3], d_xc[t % 3]
                            H2, dH2 = h2T[t % 3], d_h2[t % 3]
                            slots = pre_w.pop(t)

                            def up(j):
                                r = slots[j]
                                W = WR[r][:, 0:2048].rearrange("p (g k c) -> p g k c", g=2, k=8)
                                ba, bg = 4 + (j % 2), 6 + (j % 2)
                                for kc in range(8):
                                    k.op(PE, lambda e: e.matmul(PS[ba][:, :], lhsT=W[:, 0, kc, :], rhs=H2[:, kc, 0:512],
                                                                start=(kc == 0), stop=(kc == 7)),
                                         [d_wr[r], dH2], [PSD[ba]], signal=(kc == 7))
                                for kc in range(8):
                                    k.op(PE, lambda e: e.matmul(PS[bg][:, :], lhsT=W[:, 1, kc, :], rhs=H2[:, kc, 1:513],
                                                                start=(kc == 0), stop=(kc == 7)),
                                         [d_wr[r], dH2], [PSD[bg]], signal=(kc == 7))
                                gb, dgb = gbuf[j % 2], d_gbuf[j % 2]
                                k.op(ACT, lambda e: e.activation(out=gb[:, 2:514], in_=PS[bg][:, :], func=AF.Copy),
                                     [PSD[bg]], [dgb])
                                ab, dab = abuf[j % 3], d_abuf[j % 3]
                                k.op(ACT, lambda e: e.activation(out=ab[:], in_=PS[ba][:, :], func=AF.Copy),
                                     [PSD[ba]], [dab])
                                if i == 0:
                                    for kc in range(8):
                                        k.op(PE, lambda e: e.matmul(PS[bg][:, 0:1], lhsT=W[:, 1, kc, :], rhs=H2[:, kc, 0:1],
                                                                    start=(kc == 0), stop=(kc == 7)),
                                             [d_wr[r], dH2], [PSD[bg]], signal=(kc == 7))
                                    k.op(POOL, lambda e: e.memset(gb[:, 0:1], 0.0), [], [dgb])
                                    k.op(ACT, lambda e: e.activation(out=gb[:, 1:2], in_=PS[bg][:, 0:1], func=AF.Copy),
                                         [PSD[bg]], [dgb])
                                else:
                                    k.op(POOL, lambda e: e.tensor_copy(out=gb[:, 0:2], in_=Gsave[:, j, :]),
                                         [d_gsave[j]], [dgb])
                                tt, dtt = t1[j % 2], d_t1[j % 2]
                                k.op(ACT, lambda e: e.activation(out=tt[:], in_=gb[:, 2:514], func=AF.Copy,
                                                                 scale=fwc[:, l, j, 2:3]), [dgb, d_const], [dtt])
                                k.op(DVE, lambda e: e.scalar_tensor_tensor(out=tt[:], in0=gb[:, 0:512], scalar=fwc[:, l, j, 0:1],
                                                                           in1=tt[:], op0=ALU.mult, op1=ALU.add),
                                     [dgb, dtt, d_const], [dtt])
                                k.op(DVE, lambda e: e.scalar_tensor_tensor(out=tt[:], in0=gb[:, 1:513], scalar=fwc[:, l, j, 1:2],
                                                                           in1=tt[:], op0=ALU.mult, op1=ALU.add),
                                     [dgb, dtt, d_const], [dtt])
                                k.op(POOL, lambda e: e.tensor_copy(out=Gsave[:, j, :], in_=gb[:, 512:514]),
                                     [dgb], [d_gsave[j]])
                                gg, dgg = ge[j % 2], d_ge[j % 2]
                                k.op(ACT, lambda e: e.activation(out=gg[:], in_=tt[:], func=AF.Gelu_apprx_tanh),
                                     [dtt], [dgg])
                                k.op(DVE, lambda e: e.tensor_tensor(out=mT[:, j, :], in0=ab[:], in1=gg[:], op=ALU.mult),
                                     [dab, dgg], [d_mT[j]])

                            def down(j):
                                r = slots[j]
                                for sub in range(4):
                                    k.op(PE, lambda e: e.matmul(PS[sub][:, :], lhsT=mT[:, j, sub * 128:(sub + 1) * 128],
                                                                rhs=WR[r][:, 2048:2560], start=(j == 0), stop=(j == NJ - 1)),
                                         [d_mT[j], d_wr[r]], [PSD[sub]], signal=(j == NJ - 1 or sub == 3))

                            up(0)
                            up(1)
                            for j in range(NJ):
                                if j + 2 < NJ:
                                    up(j + 2)
                                down(j)
                                if j + RING < NJ:
                                    slots[j + RING] = load_w(j + RING)
                                if j == NJ - 6:
                                    prefetch_w2(t)
                                if j % 3 == 2:
                                    flush_loads(1)
                            flush_loads()
                            if t + 1 < NTILE:
                                prefetch_w(t + 1)

                        def evac_y(t, hf):
                            s, i, nt = tile_seq[t]
                            X, dX = XC[t % 3], d_xc[t % 3]
                            G_, dG = GATE[s % 2], d_gatesb[s % 2]
                            for sub in range(4):
                                tm, dtm = tmpo[sub % 2], d_tmpo[sub % 2]
                                k.op(DVE, lambda e: e.tensor_tensor(out=tm[:], in0=PS[sub][:, :],
                                                                    in1=G_[:, 1, hf * 512:(hf + 1) * 512], op=ALU.mult),
                                     [PSD[sub], dG], [dtm])
                                k.op(POOL, lambda e: e.tensor_tensor(out=X[:, sub, hf * 512:(hf + 1) * 512],
                                                                     in0=X[:, sub, hf * 512:(hf + 1) * 512],
                                                                     in1=tm[:], op=ALU.add), [dtm, dX], [dX])

                        def stage2b(t):
                            s, i, nt = tile_seq[t]
                            X, dX = XC[t % 3], d_xc[t % 3]
                            slots = pre_w2.pop(t)
                            for j in range(NJ):
                                if j + RING - 1 < NJ:
                                    slots[j + RING - 1] = load_w2(j + RING - 1)
                                r = slots[j]
                                for sub in range(4):
                                    k.op(PE, lambda e: e.matmul(PS[sub][:, :], lhsT=mT[:, j, sub * 128:(sub + 1) * 128],
                                                                rhs=WR2[r][:, :], start=(j == 0), stop=(j == NJ - 1)),
                                         [d_mT[j], d_wr2[r]], [PSD[sub]], signal=(j == NJ - 1 or sub == 3))

                        def stage2b_tail(t):
                            X, dX = XC[t % 3], d_xc[t % 3]
                            evac_y(t, 1)
                            t0 = t * 512
                            k.dma(SP, xdst[t0:t0 + 512, :].rearrange("(s p) f -> p s f", p=128), X[:],
                                  [dX], [k.ddep(xdst_name, t)], ds_xo[t % 3])

                        load_s1(0)
                        if NTILE > 1:
                            load_s1(1)
                        stage1a(0)
                        stage1b(0)
                        if NTILE > 1:
                            if NTILE > 2:
                                load_s1(2)
                            stage1a(1)
                            stage1b(1)
                        prefetch_w(0)
                        for t in range(NTILE):
                            stage2a(t)
                            evac_y(t, 0)
                            if t + 2 < NTILE:
                                stage1a(t + 2)
                            stage2b(t)
                            if t + 2 < NTILE:
                                stage1b(t + 2)
                            stage2b_tail(t)
                            if t + 3 < NTILE:
                                load_s1(t + 3, deferred=True)
                        k.barrier()
        except _Stop:
            pass
        k.dead = False
        k.final_wait(SP)
    return nc


def _host_inputs(inputs, seqs_per_core=None):
    f = lambda a: np.ascontiguousarray(np.asarray(a, dtype=np.float32))
    xp, xs = f(inputs["x_prompt"]), f(inputs["x_sample"])
    cp, cs = f(inputs["c_prompt"]), f(inputs["c_sample"])
    oh, vm = _static_tables()
    ng = np.stack([f(inputs["norm1_g"]), f(inputs["norm2_g"])], 1)
    ng = np.ascontiguousarray(ng.reshape(DEPTH, 2, 8, 128).transpose(3, 0, 1, 2))
    bada = f(inputs["b_ada"])
    badac = np.ascontiguousarray(bada.reshape(DEPTH, 48, 128).transpose(2, 0, 1))
    wada = np.ascontiguousarray(f(inputs["w_ada"]).reshape(DEPTH, 8, 128, 12, 512).transpose(0, 3, 2, 1, 4))
    win = np.ascontiguousarray(f(inputs["w_in"]).reshape(DEPTH, 8, 128, DIN).transpose(0, 2, 1, 3))
    qg, kg = f(inputs["q_norm_g"]), f(inputs["k_norm_g"])
    qkg = np.stack([np.tile(qg, (1, 2)), np.tile(kg, (1, 2))], 2)
    qkg = np.ascontiguousarray(qkg.transpose(1, 0, 2))
    cw = np.ascontiguousarray(f(inputs["conv_dw_w"]).reshape(DEPTH, CK, 4, 128).transpose(3, 0, 2, 1))
    cb = np.stack([f(inputs["conv_dw_b"]), f(inputs["conv_ln_g"]), f(inputs["conv_ln_b"])], 1)
    cb = np.ascontiguousarray(cb.reshape(DEPTH, 3, 4, 128).transpose(3, 0, 1, 2))
    wout = np.ascontiguousarray(f(inputs["w_out"]).reshape(DEPTH, 8, 128, D).transpose(0, 2, 1, 3))
    wu = f(inputs["w_up"]).reshape(DEPTH, 8, 128, 2, NJ, 128)
    wup = np.ascontiguousarray(wu.transpose(0, 4, 2, 3, 1, 5)).reshape(DEPTH, NJ, 128, 2048)
    wdn = np.ascontiguousarray(f(inputs["w_down"]).reshape(DEPTH, NJ, 128, D))
    fw = np.ascontiguousarray(f(inputs["ffn_dw_w"]).reshape(DEPTH, 3, NJ, 128).transpose(3, 0, 2, 1))
    shared = dict(relb=f(inputs["rel_bias"]), oh=oh, vm=vm, ng=ng, badac=badac, bada=bada, wada=wada, win=win,
                  qkg=qkg, cw=cw, cb=cb, wout=wout, wup=wup, wdn=wdn, fw=fw)
    in_maps = []
    for c in range(NCORES):
        xin = np.concatenate([xp[c], xs[2 * c], xs[2 * c + 1]], 0)
        cc = np.stack([cp[c], cs[2 * c], cs[2 * c + 1]], 0)
        cT = np.ascontiguousarray(cc.reshape(3, 8, 128).transpose(2, 1, 0))
        m = dict(shared)
        m["xin"] = np.ascontiguousarray(xin)
        m["cT"] = cT
        in_maps.append(m)
    return in_maps


_NC_CACHE = {}


def kernel(**inputs):
    in_maps = _host_inputs(inputs)
    if "nc" not in _NC_CACHE:
        _NC_CACHE["nc"] = build_program()
    nc = _NC_CACHE["nc"]
    res = run_bass_kernel_spmd(nc, in_maps, core_ids=list(range(NCORES)))
    ys = [np.asarray(r["y"], dtype=np.float32) for r in res.results]
    y_prompt = np.stack([ys[c][0:4096] for c in range(NCORES)], 0)
    y_sample = np.stack([ys[c // 2][4096 + 2048 * (c % 2):4096 + 2048 * (c % 2 + 1)] for c in range(2 * NCORES)], 0)
    return (y_prompt, y_sample)
```

```python
import math
import os
from contextlib import ExitStack

_DBG = os.environ.get('DBG_B', 'all')
_LV = {'load': 0, 'qk': 1, 'exp': 2, 'mul': 3, 'pv': 4, 'all': 5}[_DBG]

import numpy as np

import concourse.bass as bass
import concourse.mybir as mybir
from concourse.bass_utils import run_bass_kernel_spmd

F32 = mybir.dt.float32
BF16 = mybir.dt.bfloat16
AF = mybir.ActivationFunctionType
ALU = mybir.AluOpType

D = 1024
DEPTH = 4
HEAD = 64
DFF = 2816
NJ = 22
DIN = 2560
CK = 31
PATTERNS = ((128, 1), (512, 4), (2048, 16))
N_BUCKETS = 32
REL_MAX_DIST = 1024
EPS = 1e-6
NCORES = 8


def _t5_bucket(rel):
    n = -rel
    half = N_BUCKETS // 2
    ret = (n < 0).astype(np.int32) * half
    n = np.abs(n)
    max_exact = half // 2
    large = max_exact + (np.log(np.maximum(n, 1) / max_exact) / np.log(REL_MAX_DIST / max_exact)
                         * (half - max_exact)).astype(np.int32)
    large = np.minimum(large, half - 1)
    return (ret + np.where(n < max_exact, n, large)).astype(np.int32)


def _static_tables():
    oh = np.zeros((32, 6, 256), np.float32)
    vm = np.zeros((8, 6, 256), np.float32)
    for pi, (_, d) in enumerate(PATTERNS):
        for blk in range(2):
            for n in range(255):
                delta = 127 - n
                if blk == 0:
                    rel = delta - 64
                    valid = delta >= 0
                else:
                    rel = delta + 64
                    valid = delta <= 0
                if valid:
                    b = int(_t5_bucket(np.array([rel * d]))[0])
                    oh[b, pi * 2 + blk, n] = 1.0
                    vm[:, pi * 2 + blk, n] = 1.0
    return oh.reshape(32, 1536), vm.reshape(8, 1536)


class Dep:
    __slots__ = ("w", "r")

    def __init__(self):
        self.w = None
        self.r = {}


class EngW:
    def __init__(self, e, sem, name):
        self.e = e
        self.sem = sem
        self.name = name
        self.n = 0
        self.seen = {}


class DSem:
    def __init__(self, sem):
        self.sem = sem
        self.n = 0


class K:
    def __init__(self, nc, es):
        self.nc = nc
        self.es = es
        self.nsem = 0
        self.PE = self._eng(nc.tensor, "pe")
        self.ACT = self._eng(nc.scalar, "act")
        self.DVE = self._eng(nc.vector, "dve")
        self.POOL = self._eng(nc.gpsimd, "pool")
        self.SP = self._eng(nc.sync, "sp")
        self.engs = [self.PE, self.ACT, self.DVE, self.POOL, self.SP]
        self.dsems = []
        self.ddeps = {}
        self.dead = False
        self.dcache = {}

    def _eng(self, e, name):
        return EngW(e, self.es.enter_context(self.nc.semaphore("s_" + name)), name)

    def dsem(self, name):
        if name in self.dcache:
            return self.dcache[name]
        d = DSem(self.es.enter_context(self.nc.semaphore("d_" + name)))
        self.dsems.append(d)
        self.dcache[name] = d
        return d

    def ddep(self, name, idx):
        key = (name, idx)
        if key not in self.ddeps:
            self.ddeps[key] = Dep()
        return self.ddeps[key]

    def _waits(self, E, reads, writes):
        waits = {}

        def need(ev, same_ok):
            if ev is None:
                return
            sem, val, owner = ev
            if owner is E and same_ok:
                return
            key = id(sem)
            if E.seen.get(key, 0) >= val:
                return
            if key not in waits or waits[key][1] < val:
                waits[key] = (sem, val)

        for b in reads:
            need(b.w, False)
        for b in writes:
            need(b.w, True)
            for ev in b.r.values():
                need(ev, True)
        for key, (sem, val) in waits.items():
            E.e.wait_ge(sem, val)
            E.seen[key] = val

    def op(self, E, fn, reads=(), writes=(), signal=True):
        if self.dead:
            return None
        self._waits(E, reads, writes)
        ins = fn(E.e)
        if signal:
            ins.then_inc(E.sem, 1)
            E.n += 1
            tick = E.n
        else:
            tick = E.n + 1
        ev = (E.sem, tick, E)
        for b in reads:
            b.r[id(E)] = ev
        for b in writes:
            b.w = ev
            b.r = {}
        return ins

    def dma(self, Q, out, in_, reads, writes, ds):
        if self.dead:
            return None
        self._waits(Q, reads, writes)
        ins = Q.e.dma_start(out=out, in_=in_)
        ins.then_inc(ds.sem, 16)
        ds.n += 16
        ev = (ds.sem, ds.n, None)
        for b in reads:
            b.r[id(ds)] = ev
        for b in writes:
            b.w = ev
            b.r = {}
        return ins

    def barrier(self):
        if self.dead:
            return
        for E in self.engs:
            for X in self.engs:
                if X is E or X.n == 0:
                    continue
                if E.seen.get(id(X.sem), 0) < X.n:
                    E.e.wait_ge(X.sem, X.n)
                    E.seen[id(X.sem)] = X.n
            for d in self.dsems:
                if d.n and E.seen.get(id(d.sem), 0) < d.n:
                    E.e.wait_ge(d.sem, d.n)
                    E.seen[id(d.sem)] = d.n

    def final_wait(self, E):
        for d in self.dsems:
            if d.n and E.seen.get(id(d.sem), 0) < d.n:
                E.e.wait_ge(d.sem, d.n)
                E.seen[id(d.sem)] = d.n
        for X in self.engs:
            if X is E or X.n == 0:
                continue
            if E.seen.get(id(X.sem), 0) < X.n:
                E.e.wait_ge(X.sem, X.n)
                E.seen[id(X.sem)] = X.n


class _Stop(Exception):
    pass


def build_program(seqs=(4096, 2048, 2048), depth=DEPTH, debug=False, stop=None):
    NSEQ = len(seqs)
    NTOK = sum(seqs)
    SMAX = max(seqs)
    seq_base = [sum(seqs[:i]) for i in range(NSEQ)]
    nc = bass.Bass("TRN2", target_bir_lowering=False)
    skind = "ExternalOutput" if debug else "Internal"

    def din(name, shape, dt=F32):
        return nc.dram_tensor(name, list(shape), dt, kind="ExternalInput")

    def dscr(name, shape, dt):
        return nc.dram_tensor(name, list(shape), dt, kind=skind)

    xin_t = din("xin", [NTOK, D])
    cT_in = din("cT", [128, 8, NSEQ])
    relb_in = din("relb", [32, 8])
    oh_in = din("oh", [32, 1536])
    vm_in = din("vm", [8, 1536])
    ng_in = din("ng", [128, DEPTH, 2, 8])
    badac_in = din("badac", [128, DEPTH, 48])
    bada_in = din("bada", [DEPTH, 6 * D])
    wada_in = din("wada", [DEPTH, 12, 128, 8, 512])
    win_in = din("win", [DEPTH, 128, 8, DIN])
    qkg_in = din("qkg", [128, DEPTH, 2])
    cw_in = din("cw", [128, DEPTH, 4, CK])
    cb_in = din("cb", [128, DEPTH, 3, 4])
    wout_in = din("wout", [DEPTH, 128, 8, D])
    wup_in = din("wup", [DEPTH, NJ, 128, 2048])
    wdn_in = din("wdn", [DEPTH, NJ, 128, D])
    fw_in = din("fw", [128, DEPTH, NJ, 3])
    y_t = nc.dram_tensor("y", [NTOK, D], F32, kind="ExternalOutput")

    xres_t = dscr("xres", [NTOK, D], F32)
    qkT_t = dscr("qkT", [8, 128, NTOK], BF16)
    v_t = dscr("vS", [NTOK, 512], BF16)
    cTs_t = dscr("cTs", [4, 128, NTOK], BF16)
    attT_t = dscr("attT", [4, 128, NTOK], BF16)
    gscr_t = dscr("gscr", [8, 1536], F32)
    escr_t = dscr("escr", [4, 3, 128, 512], BF16)
    gate_t = dscr("gates", [DEPTH, 2, NSEQ, D], F32)
    wS_t = dscr("wS", [DEPTH, NJ, 128, 2560], BF16)
    wS2_t = dscr("wS2", [DEPTH, NJ, 128, 512], BF16)

    xin, y = xin_t.ap(), y_t.ap()
    xres, qkT, vS, cTs, attT = xres_t.ap(), qkT_t.ap(), v_t.ap(), cTs_t.ap(), attT_t.ap()
    gscr, escr, gates, wS, wS2 = gscr_t.ap(), escr_t.ap(), gate_t.ap(), wS_t.ap(), wS2_t.ap()
    wada, win, wout, wup, wdn = wada_in.ap(), win_in.ap(), wout_in.ap(), wup_in.ap(), wdn_in.ap()

    with ExitStack() as es:
        k = K(nc, es)
        PE, ACT, DVE, POOL, SP = k.PE, k.ACT, k.DVE, k.POOL, k.SP

        _cnt = [0]

        def sb(st, name, shape, dt):
            _cnt[0] += 1
            return st.enter_context(nc.sbuf_tensor(f"sb{_cnt[0]}_{name}", list(shape), dt))

        PS = [es.enter_context(nc.psum_tensor(f"ps{b}", [128, 512], F32)) for b in range(8)]
        PSD = [Dep() for _ in range(8)]

        def psb(b):
            return PS[b][:, :].bitcast(BF16)

        ident = sb(es, "ident", [128, 128], BF16)
        Jm = sb(es, "Jm", [128, 128], F32)
        ones_bf = sb(es, "ones_bf", [128, 128], BF16)
        blk1 = sb(es, "blk1", [128, 128], BF16)
        onesd = sb(es, "onesd", [128, 128], F32)
        scT = sb(es, "scT", [128, 8, NSEQ], F32)
        ngc = sb(es, "ngc", [128, DEPTH, 2, 8], F32)
        badac = sb(es, "badac", [128, DEPTH, 48], F32)
        qkg = sb(es, "qkg", [128, DEPTH, 2], F32)
        cwc = sb(es, "cwc", [128, DEPTH, 4, CK], F32)
        cbc = sb(es, "cbc", [128, DEPTH, 3, 4], F32)
        fwc = sb(es, "fwc", [128, DEPTH, NJ, 3], F32)
        junkA = sb(es, "junkA", [128, 1024], BF16)
        lnb_qk = sb(es, "lnb_qk", [128, 2], F32)
        d_const = Dep()
        ds_misc = k.dsem("misc")
        ds_pre = [k.dsem(f"pre{l_}") for l_ in range(DEPTH)]

        d_wS = [Dep() for _ in range(DEPTH)]

        def precast(l):
            for j in range(NJ):
                k.dma(POOL, wS[l, j, :, 0:2048], wup[l, j, :, :], [], [d_wS[l]], ds_pre[l])
                k.dma(POOL, wS[l, j, :, 2048:2560], wdn[l, j, :, 0:512], [], [d_wS[l]], ds_pre[l])
                k.dma(POOL, wS2[l, j, :, :], wdn[l, j, :, 512:1024], [], [d_wS[l]], ds_pre[l])

        k.op(POOL, lambda e: e.memset(ident[:], 0.0), [], [d_const])
        k.op(POOL, lambda e: e.affine_select(out=ident[:], in_=ident[:], pattern=[[-1, 128]],
                                             compare_op=ALU.not_equal, fill=1.0, base=0,
                                             channel_multiplier=1), [d_const], [d_const])
        k.op(POOL, lambda e: e.memset(Jm[:], 0.0), [], [d_const])
        k.op(POOL, lambda e: e.affine_select(out=Jm[:], in_=Jm[:], pattern=[[1, 128]],
                                             compare_op=ALU.not_equal, fill=1.0, base=-127,
                                             channel_multiplier=1), [d_const], [d_const])
        k.op(DVE, lambda e: e.memset(ones_bf[:], 1.0), [], [d_const])
        k.op(DVE, lambda e: e.memset(lnb_qk[:, 0:1], 64.0 * EPS), [], [d_const])
        k.op(DVE, lambda e: e.memset(lnb_qk[:, 1:2], EPS), [], [d_const])
        k.op(DVE, lambda e: e.memset(blk1[:], 0.0), [], [d_const])
        k.op(DVE, lambda e: e.memset(blk1[0:64, 0:64], 1.0), [], [d_const])
        k.op(DVE, lambda e: e.memset(blk1[64:128, 64:128], 1.0), [], [d_const])
        k.op(DVE, lambda e: e.memset(onesd[:], 1.0 / 512.0), [], [d_const])
        for (dst, src) in ((scT, cT_in), (ngc, ng_in), (badac, badac_in), (qkg, qkg_in), (cwc, cw_in),
                           (cbc, cb_in), (fwc, fw_in)):
            k.dma(SP, dst[:], src.ap(), [], [d_const], ds_misc)
        k.op(ACT, lambda e: e.activation(out=scT[:], in_=scT[:], func=AF.Silu), [d_const], [d_const])
        scTb = sb(es, "scTb", [128, 8, NSEQ], BF16)
        k.op(DVE, lambda e: e.tensor_copy(out=scTb[:], in_=scT[:]), [d_const], [d_const])

        with ExitStack() as st:
            relb = sb(st, "relb", [32, 8], F32)
            ohs = sb(st, "ohs", [32, 1536], F32)
            vms = sb(st, "vms", [8, 1536], F32)
            gv = sb(st, "gv", [8, 1536], F32)
            hk = sb(st, "hk", [128, 512], F32)
            et = sb(st, "et", [128, 512], BF16)
            d_t = Dep()
            d_gv = Dep()
            d_hk, d_et = Dep(), Dep()
            ds_e1, ds_e2, ds_e3 = k.dsem("e1"), k.dsem("e2"), k.dsem("e3")
            k.dma(SP, relb[:], relb_in.ap(), [], [d_t], ds_e1)
            k.dma(SP, ohs[:], oh_in.ap(), [], [d_t], ds_e1)
            k.dma(SP, vms[:], vm_in.ap(), [], [d_t], ds_e1)
            for c3 in range(3):
                k.op(PE, lambda e: e.matmul(PS[c3][0:8, :], lhsT=relb[:, :], rhs=ohs[:, c3 * 512:(c3 + 1) * 512],
                                            start=True, stop=True), [d_t], [PSD[c3]])
                k.op(ACT, lambda e: e.activation(out=gv[:, c3 * 512:(c3 + 1) * 512], in_=PS[c3][0:8, :],
                                                 func=AF.Exp), [PSD[c3]], [d_gv])
            k.op(DVE, lambda e: e.tensor_tensor(out=gv[:], in0=gv[:], in1=vms[:], op=ALU.mult), [d_gv, d_t], [d_gv])
            d_gscr = Dep()
            k.dma(SP, gscr[:, :], gv[:], [d_gv], [d_gscr], ds_e1)
            d_escr = k.ddep("escr", 0)
            for a in range(4):
                for pi in range(3):
                    for hh in range(2):
                        for blk in range(2):
                            off = (2 * a + hh) * 1536 + (pi * 2 + blk) * 256
                            src = bass.AP(gscr_t, off, [[1, 128], [1, 128]])
                            sl = (hh * 2 + blk) * 128
                            k.dma(SP, hk[:, sl:sl + 128], src, [d_gscr], [d_hk], ds_e2)
                    k.op(PE, lambda e: e.matmul(PS[3][:, :], lhsT=Jm[:, :], rhs=hk[:, :], start=True, stop=True),
                         [d_hk, d_const], [PSD[3]])
                    k.op(DVE, lambda e: e.tensor_copy(out=et[:], in_=PS[3][:, :]), [PSD[3]], [d_et])
                    k.dma(SP, escr[a, pi, :, :], et[:], [d_et], [d_escr], ds_e3)
            k.barrier()

        def _chk(name):
            if stop == name:
                k.barrier()
                k.dead = True

        try:
          _chk('etab')
          for l in range(depth):
            xsrc = xin if l == 0 else xres
            xsrc_name = "xin" if l == 0 else "xres"
            xdst = y if l == depth - 1 else xres
            xdst_name = "y" if l == depth - 1 else "xres"
            with ExitStack() as ls:
                MODC = sb(ls, f"MODC{l}", [128, 4, 8, NSEQ], F32)
                d_modc = Dep()
                with ExitStack() as st:
                    screp = sb(st, "screp", [128, 8, NSEQ, 128], BF16)
                    WP = [sb(st, f"wp{i}", [128, 8, 512], BF16) for i in range(2)]
                    d_wp = [Dep(), Dep()]
                    ds_wp = [k.dsem(f"wp_{i}") for i in range(2)]
                    brow = sb(st, "brow", [128, 2, D], F32)
                    gst = [sb(st, f"gst{i}", [128, 512], F32) for i in range(2)]
                    d_gst = [Dep(), Dep()]
                    ds_gst = [k.dsem(f"gst_{i}") for i in range(2)]
                    d_screp, d_brow = Dep(), Dep()
                    k.op(DVE, lambda e: e.memset(screp[:], 1.0), [], [d_screp])
                    for kc in range(8):
                        for s in range(NSEQ):
                            k.op(DVE, lambda e: e.tensor_scalar(out=screp[:, kc, s, :], in0=screp[:, kc, s, :],
                                                                scalar1=scT[:, kc, s:s + 1], scalar2=None,
                                                                op0=ALU.mult), [d_screp, d_const], [d_screp])
                    for g in range(2):
                        f0 = 2048 + g * 3072
                        k.dma(SP, brow[:, g, :], bada_in.ap()[l:l + 1, f0:f0 + D].broadcast_to([128, D]),
                              [], [d_brow], ds_misc)
                    pieces = []
                    for (kind, fbase) in ((0, 0), (1, 1024), (2, 3072), (3, 4096)):
                        for hf in range(2):
                            pieces.append(("col", kind, hf, fbase + hf * 512))
                    for g in range(2):
                        for hf in range(2):
                            pieces.append(("gate", g, hf, 2048 + g * 3072 + hf * 512))
                    d_gate = k.ddep("gates", l)
                    gi = 0
                    for pi_, (typ, kind, hf, f0) in enumerate(pieces):
                        sl = pi_ % 2
                        k.dma(POOL, WP[sl][:], wada[l, f0 // 512, :, :, :], [], [d_wp[sl]], ds_wp[sl])
                        if typ == "col":
                            bank = 4 + (pi_ % 2)
                            for fc in range(4):
                                for kc in range(8):
                                    k.op(PE, lambda e: e.matmul(PS[bank][:, fc * 4:fc * 4 + NSEQ],
                                                                lhsT=WP[sl][:, kc, fc * 128:(fc + 1) * 128],
                                                                rhs=scTb[:, kc, :], start=(kc == 0), stop=(kc == 7)),
                                         [d_wp[sl], d_const], [PSD[bank]], signal=(kc == 7))
                            fc0 = f0 // 128
                            for s in range(NSEQ):
                                k.op(DVE, lambda e: e.tensor_tensor(out=MODC[:, kind, hf * 4:hf * 4 + 4, s],
                                                                    in0=PS[bank][:, s:16:4],
                                                                    in1=badac[:, l, fc0:fc0 + 4], op=ALU.add),
                                     [PSD[bank], d_const], [d_modc])
                        else:
                            for s in range(NSEQ):
                                bank = 6 + (gi % 2)
                                for kc in range(8):
                                    k.op(PE, lambda e: e.matmul(PS[bank][:, :], lhsT=screp[:, kc, s, :],
                                                                rhs=WP[sl][:, kc, :], start=(kc == 0), stop=(kc == 7)),
                                         [d_wp[sl], d_screp], [PSD[bank]], signal=(kc == 7))
                                gs = gi % 2
                                k.op(DVE, lambda e: e.tensor_tensor(out=gst[gs][:], in0=PS[bank][:, :],
                                                                    in1=brow[:, kind, hf * 512:(hf + 1) * 512],
                                                                    op=ALU.add),
                                     [PSD[bank], d_brow], [d_gst[gs]])
                                k.dma(SP, gates[l, kind, s:s + 1, hf * 512:(hf + 1) * 512], gst[gs][0:1, :],
                                      [d_gst[gs]], [d_gate], ds_gst[gs])
                                gi += 1
                    for (kind, which) in ((1, 0), (3, 1)):
                        for s in range(NSEQ):
                            k.op(DVE, lambda e: e.tensor_scalar(out=MODC[:, kind, :, s], in0=MODC[:, kind, :, s],
                                                                scalar1=1.0, scalar2=None, op0=ALU.add),
                                 [d_modc], [d_modc])
                            k.op(DVE, lambda e: e.tensor_tensor(out=MODC[:, kind, :, s], in0=MODC[:, kind, :, s],
                                                                in1=ngc[:, l, which, :], op=ALU.mult),
                                 [d_modc, d_const], [d_modc])
                    k.barrier()
                    _chk('adaln')

                with ExitStack() as As:
                    WIN = sb(As, "WIN", [128, 8, DIN], BF16)
                    U = sb(As, "U", [128, 4, SMAX + 30], BF16)
                    DG = sb(As, "DG", [128, 4, CK, 128], BF16)
                    d_win, d_dg = Dep(), Dep()
                    ds_win = k.dsem("win")
                    for kc in range(8):
                        k.dma(POOL, WIN[:, kc, :], win[l, :, kc, :], [], [d_win], ds_win)
                    for c in range(4):
                        for j in range(CK):
                            k.op(POOL, lambda e: e.tensor_scalar(out=DG[:, c, j, :], in0=ident[:],
                                                                 scalar1=cwc[:, l, c, j:j + 1], scalar2=None,
                                                                 op0=ALU.mult), [d_const], [d_dg])
                    if l == 0:
                        precast(0)
                    for s in range(NSEQ):
                        S = seqs[s]
                        T0 = seq_base[s]
                        NT = S // 512
                        d_U = [Dep() for _ in range(NT)]
                        d_Uh = Dep()
                        with ExitStack() as st:
                            XA = [sb(st, f"XA{i}", [128, 4, D], F32) for i in range(2)]
                            d_xa = [Dep(), Dep()]
                            ds_xa = [k.dsem(f"xa_{i}") for i in range(2)]
                            ybf = sb(st, "ybf", [128, 4, D], BF16)
                            d_ybf = Dep()
                            hT = [sb(st, f"hT{i}", [128, 8, 512], BF16) for i in range(2)]
                            d_hT = [Dep(), Dep()]
                            ssq = sb(st, "ssq", [128, 4], F32)
                            rstd = sb(st, "rstd", [128, 4], F32)
                            d_ssq, d_rstd = Dep(), Dep()
                            QKst = [sb(st, f"QKst{i}", [128, 8, 512], BF16) for i in range(2)]
                            d_qkst = [Dep(), Dep()]
                            ds_qk = [k.dsem(f"qk_{i}") for i in range(2)]
                            Vst = [sb(st, f"Vst{i}", [128, 4, 512], BF16) for i in range(2)]
                            d_vst = [Dep(), Dep()]
                            ds_v = [k.dsem(f"v_{i}") for i in range(2)]
                            sqb = [sb(st, f"sqb{i}", [128, 512], BF16) for i in range(2)]
                            d_sqb = [Dep(), Dep()]
                            rt = [sb(st, f"rt{i}", [128, 512], F32) for i in range(2)]
                            d_rt = [Dep(), Dep()]
                            sg = [sb(st, f"sg{i}", [128, 512], F32) for i in range(2)]
                            d_sg = [Dep(), Dep()]
                            k.op(POOL, lambda e: e.memset(U[:, :, 0:15], 0.0), [], [d_Uh])
                            k.op(POOL, lambda e: e.memset(U[:, :, 15 + S:30 + S], 0.0), [], [d_Uh])

                            def load_x(i):
                                t0 = T0 + i * 512
                                k.dma(SP, XA[i % 2][:], xsrc[t0:t0 + 512, :].rearrange("(s p) f -> p s f", p=128),
                                      [k.ddep(xsrc_name, t0 // 512)], [d_xa[i % 2]], ds_xa[i % 2])

                            rot = {"p": 0, "s": 0}

                            def fe_elem(i):
                                X = XA[i % 2]
                                dX = d_xa[i % 2]
                                k.op(DVE, lambda e: e.memset(ssq[:], 0.0), [], [d_ssq])
                                for sub in range(4):
                                    k.op(ACT, lambda e: e.activation(out=junkA[:], in_=X[:, sub, :], func=AF.Square,
                                                                     accum_out=ssq[:, sub:sub + 1]),
                                         [dX, d_ssq], [d_ssq])
                                k.op(ACT, lambda e: e.activation(out=rstd[:], in_=ssq[:], func=AF.Sqrt,
                                                                 scale=1.0 / D, bias=EPS), [d_ssq], [d_rstd])
                                k.op(DVE, lambda e: e.reciprocal(out=rstd[:], in_=rstd[:]), [d_rstd], [d_rstd])
                                for sub in range(4):
                                    if sub % 2 == 0:
                                        k.op(ACT, lambda e: e.activation(out=ybf[:, sub, :], in_=X[:, sub, :],
                                                                         func=AF.Copy, scale=rstd[:, sub:sub + 1]),
                                             [dX, d_rstd], [d_ybf])
                                    else:
                                        k.op(DVE, lambda e: e.tensor_scalar(out=ybf[:, sub, :], in0=X[:, sub, :],
                                                                            scalar1=rstd[:, sub:sub + 1], scalar2=None,
                                                                            op0=ALU.mult), [dX, d_rstd], [d_ybf])

                            def fe_pe(i):
                                H = hT[i % 2]
                                dH = d_hT[i % 2]
                                for kp in range(4):
                                    bank = kp % 2
                                    for k2 in range(2):
                                        kc = kp * 2 + k2
                                        for sub in range(4):
                                            o = (k2 * 4 + sub) * 128
                                            k.op(PE, lambda e: e.transpose(psb(bank)[:, o:o + 128],
                                                                           ybf[:, sub, kc * 128:(kc + 1) * 128],
                                                                           ident[:]),
                                                 [d_ybf, d_const], [PSD[bank]], signal=(k2 == 1 and sub == 3))
                                    for k2 in range(2):
                                        kc = kp * 2 + k2
                                        if k2 == 0:
                                            k.op(DVE, lambda e: e.tensor_scalar(out=H[:, kc, :], in0=psb(bank)[:, 0:512],
                                                                                scalar1=MODC[:, 1, kc, s:s + 1],
                                                                                scalar2=MODC[:, 0, kc, s:s + 1],
                                                                                op0=ALU.mult, op1=ALU.add),
                                                 [PSD[bank], d_modc], [dH])
                                        else:
                                            k.op(ACT, lambda e: e.activation(out=H[:, kc, :], in_=psb(bank)[:, 512:1024],
                                                                             func=AF.Identity, scale=MODC[:, 1, kc, s:s + 1],
                                                                             bias=MODC[:, 0, kc, s:s + 1]),
                                                 [PSD[bank], d_modc], [dH])

                            def proj_qk(i):
                                t0 = T0 + i * 512
                                H = hT[i % 2]
                                dH = d_hT[i % 2]
                                QK = QKst[i % 2]
                                dQK = d_qkst[i % 2]
                                banks = {}

                                def qk_norm(j):
                                    bank = banks[j]
                                    sq_ = sqb[j % 2]
                                    sbank = 6 + rot["s"] % 2
                                    rot["s"] += 1
                                    k.op(PE, lambda e: e.matmul(PS[sbank][:, :], lhsT=blk1[:, :], rhs=sq_[:],
                                                                start=True, stop=True),
                                         [d_sqb[j % 2], d_const], [PSD[sbank]])
                                    r_ = rt[j % 2]
                                    k.op(ACT, lambda e: e.activation(out=r_[:], in_=PS[sbank][:, :], func=AF.Ln,
                                                                     bias=lnb_qk[:, 0:1], scale=1.0),
                                         [PSD[sbank], d_const], [d_rt[j % 2]])
                                    k.op(ACT, lambda e: e.activation(out=r_[:], in_=r_[:], func=AF.Exp, scale=-0.5),
                                         [d_rt[j % 2]], [d_rt[j % 2]])
                                    k.op(DVE, lambda e: e.scalar_tensor_tensor(out=QK[:, j, :], in0=PS[bank][:, :],
                                                                               scalar=qkg[:, l, (j // 4):(j // 4) + 1],
                                                                               in1=r_[:], op0=ALU.mult, op1=ALU.mult),
                                         [PSD[bank], d_rt[j % 2], d_const], [dQK])

                                for j in range(8):
                                    bank = 2 + rot["p"] % 4
                                    rot["p"] += 1
                                    banks[j] = bank
                                    for kc in range(8):
                                        k.op(PE, lambda e: e.matmul(PS[bank][:, :], lhsT=WIN[:, kc, j * 128:(j + 1) * 128],
                                                                    rhs=H[:, kc, :], start=(kc == 0), stop=(kc == 7)),
                                             [d_win, dH], [PSD[bank]], signal=(kc == 7))
                                    sq_ = sqb[j % 2]
                                    k.op(ACT, lambda e: e.activation(out=sq_[:], in_=PS[bank][:, :], func=AF.Square),
                                         [PSD[bank]], [d_sqb[j % 2]])
                                    if j >= 1:
                                        qk_norm(j - 1)
                                qk_norm(7)
                                k.dma(SP, qkT[:, :, t0:t0 + 512].rearrange("j p t -> p j t"), QK[:],
                                      [dQK], [k.ddep("qkT", t0 // 512)], ds_qk[i % 2])

                            def proj_rest(i):
                                t0 = T0 + i * 512
                                H = hT[i % 2]
                                dH = d_hT[i % 2]
                                Vs = Vst[i % 2]
                                dV = d_vst[i % 2]
                                for sub in range(4):
                                    bank = 2 + rot["p"] % 4
                                    rot["p"] += 1
                                    for kc in range(8):
                                        k.op(PE, lambda e: e.matmul(PS[bank][:, :], lhsT=H[:, kc, sub * 128:(sub + 1) * 128],
                                                                    rhs=WIN[:, kc, 1024:1536], start=(kc == 0), stop=(kc == 7)),
                                             [d_win, dH], [PSD[bank]], signal=(kc == 7))
                                    k.op(ACT, lambda e: e.activation(out=Vs[:, sub, :], in_=PS[bank][:, :], func=AF.Copy),
                                         [PSD[bank]], [dV])
                                k.dma(SP, vS[t0:t0 + 512, :].rearrange("(s p) f -> p s f", p=128), Vs[:],
                                      [dV], [k.ddep("vS", t0 // 512)], ds_v[i % 2])
                                for c in range(4):
                                    bv = 2 + rot["p"] % 4
                                    rot["p"] += 1
                                    bg = 2 + rot["p"] % 4
                                    rot["p"] += 1
                                    for kc in range(8):
                                        k.op(PE, lambda e: e.matmul(PS[bv][:, :], lhsT=WIN[:, kc, 1536 + c * 128:1536 + (c + 1) * 128],
                                                                    rhs=H[:, kc, :], start=(kc == 0), stop=(kc == 7)),
                                             [d_win, dH], [PSD[bv]], signal=(kc == 7))
                                    for kc in range(8):
                                        k.op(PE, lambda e: e.matmul(PS[bg][:, :], lhsT=WIN[:, kc, 2048 + c * 128:2048 + (c + 1) * 128],
                                                                    rhs=H[:, kc, :], start=(kc == 0), stop=(kc == 7)),
                                             [d_win, dH], [PSD[bg]], signal=(kc == 7))
                                    s_ = sg[c % 2]
                                    k.op(ACT, lambda e: e.activation(out=s_[:], in_=PS[bg][:, :], func=AF.Sigmoid),
                                         [PSD[bg]], [d_sg[c % 2]])
                                    k.op(DVE, lambda e: e.tensor_tensor(out=U[:, c, 15 + i * 512:15 + (i + 1) * 512],
                                                                        in0=PS[bv][:, :], in1=s_[:], op=ALU.mult),
                                         [PSD[bv], d_sg[c % 2]], [d_U[i]])

                            load_x(0)
                            if NT > 1:
                                load_x(1)
                            fe_elem(0)
                            fe_pe(0)
                            for i in range(NT):
                                if i + 1 < NT:
                                    fe_elem(i + 1)
                                    if i + 2 < NT:
                                        load_x(i + 2)
                                proj_qk(i)
                                if i + 1 < NT:
                                    fe_pe(i + 1)
                                proj_rest(i)
                            k.barrier()
                            _chk('A')
                        with ExitStack() as st:
                            cp = sb(st, "cp", [128, 4, 512], F32)
                            sq4 = sb(st, "sq4", [128, 4, 512], F32)
                            m2 = sb(st, "m2", [128, 512], F32)
                            var = sb(st, "var", [128, 512], F32)
                            cst = [sb(st, f"cst{i}", [128, 4, 512], BF16) for i in range(2)]
                            d_cp, d_sq4, d_m2, d_var = Dep(), Dep(), Dep(), Dep()
                            d_cst = [Dep(), Dep()]
                            ds_c = [k.dsem(f"c_{i}") for i in range(2)]
                            for i in range(NT):
                                t0 = T0 + i * 512
                                ureads = [d_Uh] + [d_U[x] for x in (i - 1, i, i + 1) if 0 <= x < NT]
                                for c in range(4):
                                    bank = 2 + c
                                    for j in range(CK):
                                        k.op(PE, lambda e: e.matmul(PS[bank][:, :], lhsT=DG[:, c, j, :],
                                                                    rhs=U[:, c, i * 512 + j:i * 512 + j + 512],
                                                                    start=(j == 0), stop=(j == CK - 1)),
                                             [d_dg] + ureads, [PSD[bank]], signal=(j == CK - 1))
                                    k.op(ACT, lambda e: e.activation(out=cp[:, c, :], in_=PS[bank][:, :], func=AF.Identity,
                                                                     bias=cbc[:, l, 0, c:c + 1], scale=1.0),
                                         [PSD[bank], d_const], [d_cp])
                                    k.op(ACT, lambda e: e.activation(out=sq4[:, c, :], in_=PS[bank][:, :], func=AF.Square,
                                                                     bias=cbc[:, l, 0, c:c + 1], scale=1.0),
                                         [PSD[bank], d_const], [d_sq4])
                                for c in range(4):
                                    k.op(PE, lambda e: e.matmul(PS[6][:, :], lhsT=onesd[:, :], rhs=cp[:, c, :],
                                                                start=(c == 0), stop=(c == 3)),
                                         [d_cp, d_const], [PSD[6]], signal=(c == 3))
                                for c in range(4):
                                    k.op(PE, lambda e: e.matmul(PS[7][:, :], lhsT=onesd[:, :], rhs=sq4[:, c, :],
                                                                start=(c == 0), stop=(c == 3)),
                                         [d_sq4, d_const], [PSD[7]], signal=(c == 3))
                                k.op(ACT, lambda e: e.activation(out=m2[:], in_=PS[6][:, :], func=AF.Square),
                                     [PSD[6]], [d_m2])
                                k.op(DVE, lambda e: e.tensor_tensor(out=var[:], in0=PS[7][:, :], in1=m2[:], op=ALU.subtract),
                                     [PSD[7], d_m2], [d_var])
                                k.op(ACT, lambda e: e.activation(out=var[:], in_=var[:], func=AF.Ln, bias=lnb_qk[:, 1:2], scale=1.0),
                                     [d_var, d_const], [d_var])
                                k.op(ACT, lambda e: e.activation(out=var[:], in_=var[:], func=AF.Exp, scale=-0.5),
                                     [d_var], [d_var])
                                C = cst[i % 2]
                                for c in range(4):
                                    k.op(DVE, lambda e: e.tensor_tensor(out=cp[:, c, :], in0=cp[:, c, :], in1=PS[6][:, :],
                                                                        op=ALU.subtract), [d_cp, PSD[6]], [d_cp])
                                    k.op(DVE, lambda e: e.tensor_tensor(out=cp[:, c, :], in0=cp[:, c, :], in1=var[:],
                                                                        op=ALU.mult), [d_cp, d_var], [d_cp])
                                    k.op(ACT, lambda e: e.activation(out=C[:, c, :], in_=cp[:, c, :], func=AF.Silu,
                                                                     scale=cbc[:, l, 1, c:c + 1], bias=cbc[:, l, 2, c:c + 1]),
                                         [d_cp, d_const], [d_cst[i % 2]])
                                k.dma(SP, cTs[:, :, t0:t0 + 512].rearrange("c p t -> p c t"), C[:],
                                      [d_cst[i % 2]], [k.ddep("cTs", t0 // 512)], ds_c[i % 2])
                            k.barrier()
                            _chk('A2')

                with ExitStack() as Bs:
                    WOUT = sb(Bs, "WOUT", [128, 8, D], BF16)
                    d_wout = Dep()
                    ds_wout = k.dsem("wout")
                    for kc in range(0, 8, 4):
                        k.dma(POOL, WOUT[:, kc:kc + 4, :], wout[l, :, kc:kc + 4, :], [], [d_wout], ds_wout)
                    if l + 1 < depth:
                        precast(l + 1)
                    with ExitStack() as st:
                        QT = [sb(st, f"QT{i}", [128, 2, SMAX], BF16) for i in range(2)]
                        KT = [sb(st, f"KT{i}", [128, SMAX], BF16) for i in range(2)]
                        ET = [sb(st, f"ET{i}", [128, 3, 2, 2, 128], BF16) for i in range(2)]
                        d_qt = [Dep(), Dep()]
                        ds_qt = [k.dsem(f"qt_{i}") for i in range(2)]
                        acc = sb(st, "acc", [128, 2, SMAX], F32)
                        d_acc = Dep()
                        VB = [sb(st, f"VB{i}", [128, 9, 512], BF16) for i in range(2)]
                        d_vb = [Dep(), Dep()]
                        ds_vb = [k.dsem(f"vb_{i}") for i in range(2)]
                        ex = [sb(st, f"ex{i}", [128, 2, 2, 128], BF16) for i in range(2)]
                        d_ex = [Dep(), Dep()]
                        PT = [sb(st, f"PT{i}", [128, 2, 2, 128], BF16) for i in range(2)]
                        d_pt = [Dep(), Dep()]
                        ast = [sb(st, f"ast{i}", [128, SMAX], BF16) for i in range(1)] * 2
                        d_ast = [Dep()] * 2
                        ds_ast = [k.dsem("ast_0")] * 2
                        pairs = [(s, a) for s in range(NSEQ) for a in range(4)]
                        QTd = {d_: sb(st, f"QTd{d_}", [128, 2, SMAX], BF16) for d_ in (4, 16)}
                        KTd = {d_: sb(st, f"KTd{d_}", [128, SMAX], BF16) for d_ in (4, 16)}
                        d_qtd = {4: Dep(), 16: Dep()}
                        for i2 in range(2):
                            k.op(DVE, lambda e: e.memset(QT[i2][64:128, 0, :], 0.0), [], [d_qt[i2]])
                            k.op(DVE, lambda e: e.memset(QT[i2][0:64, 1, :], 0.0), [], [d_qt[i2]])

                        def load_pair(idx):
                            s, a = pairs[idx]
                            S, T0 = seqs[s], seq_base[s]
                            sl = idx % 2
                            rds = [k.ddep("qkT", t) for t in range(T0 // 512, (T0 + S) // 512)]
                            k.dma(SP, QT[sl][0:64, 0, 0:S], qkT[a, 0:64, T0:T0 + S], rds, [d_qt[sl]], ds_qt[sl])
                            k.dma(SP, QT[sl][64:128, 1, 0:S], qkT[a, 64:128, T0:T0 + S], rds, [d_qt[sl]], ds_qt[sl])
                            k.dma(SP, KT[sl][:, 0:S], qkT[4 + a, :, T0:T0 + S], rds, [d_qt[sl]], ds_qt[sl])
                            k.dma(SP, ET[sl][:].rearrange("p a h b c -> p a (h b c)"),
                                  escr[a, :, :, :].rearrange("a p x -> p a x"),
                                  [k.ddep("escr", 0)], [d_qt[sl]], ds_qt[sl])

                        def segments(S):
                            out = []
                            for pi, (_, d) in enumerate(PATTERNS):
                                L = S // d
                                NB = L // 128
                                for c in range(d):
                                    for m0 in range(0, NB + 1, 8):
                                        m1 = min(m0 + 8, NB + 1)
                                        b0 = max(m0 - 1, 0)
                                        b1 = min(m1 - 1, NB - 1)
                                        out.append((pi, d, c, m0, m1, b0, b1 - b0 + 1, NB))
                            return out

                        LOOK = 5
                        NSB = LOOK + 1
                        exs = ex + [sb(st, f"exx{i}", [128, 2, 2, 128], BF16) for i in range(NSB - 2)]
                        d_exs = d_ex + [Dep() for _ in range(NSB - 2)]
                        PTs = PT + [sb(st, f"PTx{i}", [128, 2, 2, 128], BF16) for i in range(NSB - 2)]
                        d_pts = d_pt + [Dep() for _ in range(NSB - 2)]
                        cnt = {"seg": 0, "g": 0}
                        load_pair(0)
                        for idx, (s, a) in enumerate(pairs):
                            S, T0 = seqs[s], seq_base[s]
                            sl = idx % 2
                            if idx + 1 < len(pairs):
                                load_pair(idx + 1)
                            segs = segments(S)
                            seg_slot = {}

                            def load_v(si):
                                (pi, d, c, m0, m1, b0, nb, NB) = segs[si]
                                slot = cnt["seg"] % 2
                                cnt["seg"] += 1
                                seg_slot[si] = slot
                                r0 = T0 + c + d * 128 * b0
                                r1 = r0 + d * 128 * nb
                                rds = [k.ddep("vS", t) for t in range(r0 // 512, min((r1 - 1) // 512 + 1, (T0 + S) // 512))]
                                src = vS[r0:r1 - d + 1:d, :] if d > 1 else vS[r0:r1, :]
                                k.dma(SP, VB[slot][:, 0:nb, :], src.rearrange("(b p) f -> p b f", p=128),
                                      rds, [d_vb[slot]], ds_vb[slot])

                            groups = []
                            for si, (pi, d, c, m0, m1, b0, nb, NB) in enumerate(segs):
                                for m in range(m0, m1):
                                    groups.append((si, m == m0, pi, d, c, m, b0, NB))
                            Q_, K_, E_ = QT[sl], KT[sl], ET[sl]
                            ginfo = {}

                            def geom(gi):
                                (si, first, pi, d, c, m, b0, NB) = groups[gi]
                                blks = []
                                if m >= 1:
                                    blks.append((0, m - 1))
                                if m <= NB - 1:
                                    blks.append((1, m))
                                qlo = 64 if m == 0 else 0
                                qhi = 64 if m == NB else 128
                                nq = qhi - qlo
                                tq0 = c + d * (128 * m - 64 + qlo)
                                tq1 = tq0 + d * (nq - 1) + 1
                                return blks, qlo, qhi, tq0, tq1

                            def emit_qk(gi):
                                (si, first, pi, d, c, m, b0, NB) = groups[gi]
                                blks, qlo, qhi, tq0, tq1 = geom(gi)
                                gq = cnt["g"] % NSB
                                cnt["g"] += 1
                                ginfo[gi] = gq
                                sbank = gq
                                SP4 = PS[sbank][:, :].rearrange("p (h b c) -> p h b c", h=2, b=2)
                                L_ = S // d
                                for hh in range(2):
                                    for (blk, bidx) in blks:
                                        if d == 1:
                                            tk0 = 128 * bidx
                                            lhs_ap = K_[:, tk0:tk0 + 128]
                                            rhs_ap = Q_[:, hh, tq0:tq1]
                                            rds = [d_qt[sl]]
                                        else:
                                            kb = c * L_ + 128 * bidx
                                            qb = c * L_ + 128 * m - 64 + qlo
                                            lhs_ap = KTd[d][:, kb:kb + 128]
                                            rhs_ap = QTd[d][:, hh, qb:qb + (qhi - qlo)]
                                            rds = [d_qtd[d]]
                                        k.op(PE, lambda e: e.matmul(SP4[:, hh, blk, qlo:qhi], lhsT=lhs_ap, rhs=rhs_ap,
                                                                    start=True, stop=True),
                                             rds, [PSD[sbank]])
                                bsel = slice(blks[0][0], blks[-1][0] + 1)
                                EX, P_ = exs[gq], PTs[gq]
                                k.op(ACT, lambda e: e.activation(out=EX[:, :, bsel, qlo:qhi], in_=SP4[:, :, bsel, qlo:qhi],
                                                                 func=AF.Exp, scale=8.0),
                                     [PSD[sbank]], [d_exs[gq]])
                                k.op(DVE, lambda e: e.tensor_tensor(out=P_[:, :, bsel, qlo:qhi], in0=EX[:, :, bsel, qlo:qhi],
                                                                    in1=E_[:, pi, :, bsel, qlo:qhi], op=ALU.mult),
                                     [d_exs[gq], d_qt[sl]], [d_pts[gq]])

                            def emit_pv(gi):
                                (si, first, pi, d, c, m, b0, NB) = groups[gi]
                                if first and si + 1 < len(segs):
                                    load_v(si + 1)
                                blks, qlo, qhi, tq0, tq1 = geom(gi)
                                gq = ginfo.pop(gi)
                                P_ = PTs[gq]
                                vsl = seg_slot[si]
                                V, dV = VB[vsl], d_vb[vsl]
                                obank = 6 + gi % 2
                                OD = PS[obank][:, 0:256].rearrange("p (t c) -> p t c", t=2)
                                for hh in range(2):
                                    for bi, (blk, bidx) in enumerate(blks):
                                        k.op(PE, lambda e: e.matmul(OD[hh * 64:(hh + 1) * 64, 0, qlo:qhi],
                                                                    lhsT=V[:, bidx - b0, a * 128 + hh * 64:a * 128 + (hh + 1) * 64],
                                                                    rhs=P_[:, hh, blk, qlo:qhi],
                                                                    start=(bi == 0), stop=(bi == len(blks) - 1)),
                                             [dV, d_pts[gq]], [PSD[obank]], signal=False)
                                    for bi, (blk, bidx) in enumerate(blks):
                                        last = (hh == 1 and bi == len(blks) - 1)
                                        k.op(PE, lambda e: e.matmul(OD[hh * 64:(hh + 1) * 64, 1, qlo:qhi],
                                                                    lhsT=ones_bf[:, 0:64],
                                                                    rhs=P_[:, hh, blk, qlo:qhi],
                                                                    start=(bi == 0), stop=(bi == len(blks) - 1)),
                                             [d_const, d_pts[gq]], [PSD[obank]], signal=last)
                                if pi == 0:
                                    k.op(DVE, lambda e: e.tensor_copy(out=acc[:, :, tq0:tq1], in_=OD[:, :, qlo:qhi]),
                                         [PSD[obank]], [d_acc])
                                else:
                                    k.op(DVE, lambda e: e.tensor_tensor(out=acc[:, :, tq0:tq1:d], in0=OD[:, :, qlo:qhi],
                                                                        in1=acc[:, :, tq0:tq1:d], op=ALU.add),
                                         [PSD[obank], d_acc], [d_acc])

                            load_v(0)
                            NG = len(groups)
                            copies = []
                            for d_ in (4, 16):
                                for hh_ in range(2):
                                    copies.append((d_, QTd[d_][:, hh_, 0:S], Q_[:, hh_, 0:S]))
                                copies.append((d_, KTd[d_][:, 0:S], K_[:, 0:S]))

                            def emit_copy():
                                d_, dst, src = copies.pop(0)
                                k.op(DVE, lambda e: e.tensor_copy(out=dst.rearrange("p (c i) -> p c i", c=d_),
                                                                  in_=src.rearrange("p (i c) -> p c i", c=d_)),
                                     [d_qt[sl]], [d_qtd[d_]])

                            for gi in range(NG + LOOK):
                                if gi < NG:
                                    if groups[gi][2] == 0:
                                        if copies and gi % 2 == 1:
                                            emit_copy()
                                    else:
                                        while copies:
                                            emit_copy()
                                    emit_qk(gi)
                                if gi - LOOK >= 0:
                                    emit_pv(gi - LOOK)
                            A_ = ast[idx % 2]
                            for t in range(S // 512):
                                cs = slice(t * 512, (t + 1) * 512)
                                k.op(ACT, lambda e: e.activation(out=acc[:, 1, cs], in_=acc[:, 1, cs], func=AF.Ln), [d_acc], [d_acc])
                                k.op(ACT, lambda e: e.activation(out=acc[:, 1, cs], in_=acc[:, 1, cs], func=AF.Exp, scale=-1.0),
                                     [d_acc], [d_acc])
                                k.op(DVE, lambda e: e.tensor_tensor(out=A_[:, cs], in0=acc[:, 0, cs], in1=acc[:, 1, cs],
                                                                    op=ALU.mult), [d_acc], [d_ast[idx % 2]])
                            k.dma(SP, attT[a, :, T0:T0 + S], A_[:, 0:S], [d_ast[idx % 2]],
                                  [k.ddep("attT", t) for t in range(T0 // 512, (T0 + S) // 512)], ds_ast[idx % 2])
                        k.barrier()
                        _chk('B')

                    with ExitStack() as st:
                        NTILE = NTOK // 512
                        tile_seq = []
                        for s in range(NSEQ):
                            for i in range(seqs[s] // 512):
                                tile_seq.append((s, i, seqs[s] // 512))
                        GATE = [sb(st, f"GATE{i}", [128, 2, D], F32) for i in range(2)]
                        d_gatesb = [Dep(), Dep()]
                        ds_gate = [k.dsem(f"gate_{i}") for i in range(2)]
                        RING = 4
                        WR = [sb(st, f"WR{i}", [128, 2560], BF16) for i in range(RING)]
                        d_wr = [Dep() for _ in range(RING)]
                        ds_wr = [k.dsem(f"wr_{i}") for i in range(RING)]
                        WR2 = [sb(st, f"WR2{i}", [128, 512], BF16) for i in range(RING)]
                        d_wr2 = [Dep() for _ in range(RING)]
                        ds_wr2 = [k.dsem(f"wr2_{i}") for i in range(RING)]
                        mT = sb(st, "mT", [128, NJ, 512], BF16)
                        d_mT = [Dep() for _ in range(NJ)]
                        XC = [sb(st, f"XC{i}", [128, 4, D], F32) for i in range(3)]
                        d_xc = [Dep() for _ in range(3)]
                        ds_xc = [k.dsem(f"xc_{i}") for i in range(3)]
                        ds_xo = [k.dsem(f"xo_{i}") for i in range(3)]
                        h2T = [sb(st, f"h2T{i}", [128, 8, 513], BF16) for i in range(3)]
                        d_h2 = [Dep() for _ in range(3)]
                        CAT = [sb(st, f"CAT{i}", [128, 8, 512], BF16) for i in range(2)]
                        d_cat = [Dep(), Dep()]
                        ds_cat = [k.dsem(f"cat_{i}") for i in range(2)]
                        ybf2 = sb(st, "ybf2", [128, 4, D], BF16)
                        d_ybf2 = Dep()
                        ssq2 = sb(st, "ssq2", [128, 4], F32)
                        rstd2 = sb(st, "rstd2", [128, 4], F32)
                        d_ssq2, d_rstd2 = Dep(), Dep()
                        tmpo = [sb(st, f"tmpo{i}", [128, 512], F32) for i in range(2)]
                        d_tmpo = [Dep(), Dep()]
                        gbuf = [sb(st, f"gbuf{i}", [128, 514], F32) for i in range(2)]
                        d_gbuf = [Dep(), Dep()]
                        t1 = [sb(st, f"t1{i}", [128, 512], F32) for i in range(2)]
                        d_t1 = [Dep(), Dep()]
                        ge = [sb(st, f"ge{i}", [128, 512], F32) for i in range(2)]
                        d_ge = [Dep(), Dep()]
                        Gsave = sb(st, "Gsave", [128, NJ, 2], F32)
                        d_gsave = [Dep() for _ in range(NJ)]
                        junkB = junkA
                        abuf = [sb(st, f"abuf{i}", [128, 512], F32) for i in range(3)]
                        d_abuf = [Dep() for _ in range(3)]

                        gate_loaded = {}

                        def load_gate(s):
                            if s in gate_loaded:
                                return
                            sl = s % 2
                            gate_loaded[s] = sl
                            for g in range(2):
                                k.dma(SP, GATE[sl][:, g, :], gates[l, g, s:s + 1, :].broadcast_to([128, D]),
                                      [k.ddep("gates", l)], [d_gatesb[sl]], ds_gate[sl])

                        pend_loads = []

                        def load_s1(t, deferred=False):
                            s, i, nt = tile_seq[t]
                            load_gate(s)
                            t0 = t * 512
                            pieces = []
                            for sub in range(4):
                                pieces.append(lambda sub=sub: k.dma(
                                    SP, XC[t % 3][:, sub, :], xsrc[t0 + sub * 128:t0 + (sub + 1) * 128, :],
                                    [k.ddep(xsrc_name, t)], [d_xc[t % 3]], ds_xc[t % 3]))
                            pieces.append(lambda: k.dma(SP, CAT[t % 2][:, 0:4, :],
                                                        attT[:, :, t0:t0 + 512].rearrange("a p t -> p a t"),
                                                        [k.ddep("attT", t)], [d_cat[t % 2]], ds_cat[t % 2]))
                            pieces.append(lambda: k.dma(SP, CAT[t % 2][:, 4:8, :],
                                                        cTs[:, :, t0:t0 + 512].rearrange("a p t -> p a t"),
                                                        [k.ddep("cTs", t)], [d_cat[t % 2]], ds_cat[t % 2]))
                            if deferred:
                                pend_loads.extend(pieces)
                            else:
                                for p_ in pieces:
                                    p_()

                        def flush_loads(n=None):
                            while pend_loads and (n is None or n > 0):
                                pend_loads.pop(0)()
                                if n is not None:
                                    n -= 1

                        s1rot = [0, 0]

                        def stage1a(t):
                            s, i, nt = tile_seq[t]
                            X, dX = XC[t % 3], d_xc[t % 3]
                            C_, dC = CAT[t % 2], d_cat[t % 2]
                            G_, dG = GATE[s % 2], d_gatesb[s % 2]
                            H2, dH2 = h2T[t % 3], d_h2[t % 3]
                            k.op(POOL, lambda e: e.memset(ssq2[:], 0.0), [], [d_ssq2])
                            for sub in range(4):
                                for hf in range(2):
                                    bank = 4 + s1rot[0] % 2
                                    s1rot[0] += 1
                                    for kc in range(8):
                                        k.op(PE, lambda e: e.matmul(PS[bank][:, :], lhsT=C_[:, kc, sub * 128:(sub + 1) * 128],
                                                                    rhs=WOUT[:, kc, hf * 512:(hf + 1) * 512],
                                                                    start=(kc == 0), stop=(kc == 7)),
                                             [dC, d_wout], [PSD[bank]], signal=(kc == 7))
                                    tm = tmpo[s1rot[0] % 2]
                                    dtm = d_tmpo[s1rot[0] % 2]
                                    k.op(DVE, lambda e: e.tensor_tensor(out=tm[:], in0=PS[bank][:, :],
                                                                        in1=G_[:, 0, hf * 512:(hf + 1) * 512], op=ALU.mult),
                                         [PSD[bank], dG], [dtm])
                                    k.op(POOL, lambda e: e.tensor_tensor(out=X[:, sub, hf * 512:(hf + 1) * 512],
                                                                         in0=X[:, sub, hf * 512:(hf + 1) * 512],
                                                                         in1=tm[:], op=ALU.add), [dtm, dX], [dX])
                                k.op(ACT, lambda e: e.activation(out=junkB[:], in_=X[:, sub, :], func=AF.Square,
                                                                 accum_out=ssq2[:, sub:sub + 1]), [dX, d_ssq2], [d_ssq2])
                            k.op(ACT, lambda e: e.activation(out=rstd2[:], in_=ssq2[:], func=AF.Sqrt, scale=1.0 / D, bias=EPS),
                                 [d_ssq2], [d_rstd2])
                            k.op(DVE, lambda e: e.reciprocal(out=rstd2[:], in_=rstd2[:]), [d_rstd2], [d_rstd2])
                            for sub in range(4):
                                k.op(ACT, lambda e: e.activation(out=ybf2[:, sub, :], in_=X[:, sub, :], func=AF.Copy,
                                                                 scale=rstd2[:, sub:sub + 1]), [dX, d_rstd2], [d_ybf2])

                        def stage1b(t):
                            s, i, nt = tile_seq[t]
                            H2, dH2 = h2T[t % 3], d_h2[t % 3]
                            for kp in range(4):
                                bank = 6 + kp % 2
                                for k2 in range(2):
                                    kc = kp * 2 + k2
                                    for sub in range(4):
                                        o = (k2 * 4 + sub) * 128
                                        k.op(PE, lambda e: e.transpose(psb(bank)[:, o:o + 128],
                                                                       ybf2[:, sub, kc * 128:(kc + 1) * 128], ident[:]),
                                             [d_ybf2, d_const], [PSD[bank]], signal=(k2 == 1 and sub == 3))
                                for k2 in range(2):
                                    kc = kp * 2 + k2
                                    if k2 == 0:
                                        k.op(DVE, lambda e: e.tensor_scalar(out=H2[:, kc, 0:512],
                                                                            in0=psb(bank)[:, 0:512],
                                                                            scalar1=MODC[:, 3, kc, s:s + 1],
                                                                            scalar2=MODC[:, 2, kc, s:s + 1],
                                                                            op0=ALU.mult, op1=ALU.add),
                                             [PSD[bank], d_modc], [dH2])
                                    else:
                                        k.op(ACT, lambda e: e.activation(out=H2[:, kc, 0:512], in_=psb(bank)[:, 512:1024],
                                                                         func=AF.Identity, scale=MODC[:, 3, kc, s:s + 1],
                                                                         bias=MODC[:, 2, kc, s:s + 1]),
                                             [PSD[bank], d_modc], [dH2])
                            if i > 0:
                                Hp, dHp = h2T[(t - 1) % 3], d_h2[(t - 1) % 3]
                                k.op(POOL, lambda e: e.tensor_copy(out=Hp[:, :, 512:513], in_=H2[:, :, 0:1]), [dH2], [dHp])
                            if i == nt - 1:
                                k.op(POOL, lambda e: e.memset(H2[:, :, 512:513], 0.0), [], [dH2])

                        wcount = [0, 0]

                        def load_w(j):
                            r = wcount[0] % RING
                            wcount[0] += 1
                            k.dma(SP, WR[r][:], wS[l, j, :, :], [d_wS[l]], [d_wr[r]], ds_wr[r])
                            return r

                        def load_w2(j):
                            r = wcount[1] % RING
                            wcount[1] += 1
                            k.dma(SP, WR2[r][:], wS2[l, j, :, :], [d_wS[l]], [d_wr2[r]], ds_wr2[r])
                            return r

                        pre_w, pre_w2 = {}, {}

                        def prefetch_w(t):
                            pre_w[t] = {j: load_w(j) for j in range(min(RING, NJ))}

                        def prefetch_w2(t):
                            pre_w2[t] = {j: load_w2(j) for j in range(min(RING - 1, NJ))}

                        def stage2a(t):
                            s, i, nt = tile_seq[t]
                            X, dX = XC[t % 3], d_xc[t % 3]
                            H2, dH2 = h2T[t % 3], d_h2[t % 3]
                            slots = pre_w.pop(t)

                            def up(j):
                                r = slots[j]
                                W = WR[r][:, 0:2048].rearrange("p (g k c) -> p g k c", g=2, k=8)
                                ba, bg = 4 + (j % 2), 6 + (j % 2)
                                for kc in range(8):
                                    k.op(PE, lambda e: e.matmul(PS[ba][:, :], lhsT=W[:, 0, kc, :], rhs=H2[:, kc, 0:512],
                                                                start=(kc == 0), stop=(kc == 7)),
                                         [d_wr[r], dH2], [PSD[ba]], signal=(kc == 7))
                                for kc in range(8):
                                    k.op(PE, lambda e: e.matmul(PS[bg][:, :], lhsT=W[:, 1, kc, :], rhs=H2[:, kc, 1:513],
                                                                start=(kc == 0), stop=(kc == 7)),
                                         [d_wr[r], dH2], [PSD[bg]], signal=(kc == 7))
                                gb, dgb = gbuf[j % 2], d_gbuf[j % 2]
                                k.op(ACT, lambda e: e.activation(out=gb[:, 2:514], in_=PS[bg][:, :], func=AF.Copy),
                                     [PSD[bg]], [dgb])
                                ab, dab = abuf[j % 3], d_abuf[j % 3]
                                k.op(ACT, lambda e: e.activation(out=ab[:], in_=PS[ba][:, :], func=AF.Copy),
                                     [PSD[ba]], [dab])
                                if i == 0:
                                    for kc in range(8):
                                        k.op(PE, lambda e: e.matmul(PS[bg][:, 0:1], lhsT=W[:, 1, kc, :], rhs=H2[:, kc, 0:1],
                                                                    start=(kc == 0), stop=(kc == 7)),
                                             [d_wr[r], dH2], [PSD[bg]], signal=(kc == 7))
                                    k.op(POOL, lambda e: e.memset(gb[:, 0:1], 0.0), [], [dgb])
                                    k.op(ACT, lambda e: e.activation(out=gb[:, 1:2], in_=PS[bg][:, 0:1], func=AF.Copy),
                                         [PSD[bg]], [dgb])
                                else:
                                    k.op(POOL, lambda e: e.tensor_copy(out=gb[:, 0:2], in_=Gsave[:, j, :]),
                                         [d_gsave[j]], [dgb])
                                tt, dtt = t1[j % 2], d_t1[j % 2]
                                k.op(ACT, lambda e: e.activation(out=tt[:], in_=gb[:, 2:514], func=AF.Copy,
                                                                 scale=fwc[:, l, j, 2:3]), [dgb, d_const], [dtt])
                                k.op(DVE, lambda e: e.scalar_tensor_tensor(out=tt[:], in0=gb[:, 0:512], scalar=fwc[:, l, j, 0:1],
                                                                           in1=tt[:], op0=ALU.mult, op1=ALU.add),
                                     [dgb, dtt, d_const], [dtt])
                                k.op(DVE, lambda e: e.scalar_tensor_tensor(out=tt[:], in0=gb[:, 1:513], scalar=fwc[:, l, j, 1:2],
                                                                           in1=tt[:], op0=ALU.mult, op1=ALU.add),
                                     [dgb, dtt, d_const], [dtt])
                                k.op(POOL, lambda e: e.tensor_copy(out=Gsave[:, j, :], in_=gb[:, 512:514]),
                                     [dgb], [d_gsave[j]])
                                gg, dgg = ge[j % 2], d_ge[j % 2]
                                k.op(ACT, lambda e: e.activation(out=gg[:], in_=tt[:], func=AF.Gelu_apprx_tanh),
                                     [dtt], [dgg])
                                k.op(DVE, lambda e: e.tensor_tensor(out=mT[:, j, :], in0=ab[:], in1=gg[:], op=ALU.mult),
                                     [dab, dgg], [d_mT[j]])

                            def down(j):
                                r = slots[j]
                                for sub in range(4):
                                    k.op(PE, lambda e: e.matmul(PS[sub][:, :], lhsT=mT[:, j, sub * 128:(sub + 1) * 128],
                                                                rhs=WR[r][:, 2048:2560], start=(j == 0), stop=(j == NJ - 1)),
                                         [d_mT[j], d_wr[r]], [PSD[sub]], signal=(j == NJ - 1 or sub == 3))

                            up(0)
                            up(1)
                            for j in range(NJ):
                                if j + 2 < NJ:
                                    up(j + 2)
                                down(j)
                                if j + RING < NJ:
                                    slots[j + RING] = load_w(j + RING)
                                if j == NJ - 6:
                                    prefetch_w2(t)
                                if j % 3 == 2:
                                    flush_loads(1)
                            flush_loads()
                            if t + 1 < NTILE:
                                prefetch_w(t + 1)

                        def evac_y(t, hf):
                            s, i, nt = tile_seq[t]
                            X, dX = XC[t % 3], d_xc[t % 3]
                            G_, dG = GATE[s % 2], d_gatesb[s % 2]
                            for sub in range(4):
                                tm, dtm = tmpo[sub % 2], d_tmpo[sub % 2]
                                k.op(DVE, lambda e: e.tensor_tensor(out=tm[:], in0=PS[sub][:, :],
                                                                    in1=G_[:, 1, hf * 512:(hf + 1) * 512], op=ALU.mult),
                                     [PSD[sub], dG], [dtm])
                                k.op(POOL, lambda e: e.tensor_tensor(out=X[:, sub, hf * 512:(hf + 1) * 512],
                                                                     in0=X[:, sub, hf * 512:(hf + 1) * 512],
                                                                     in1=tm[:], op=ALU.add), [dtm, dX], [dX])

                        def stage2b(t):
                            s, i, nt = tile_seq[t]
                            X, dX = XC[t % 3], d_xc[t % 3]
                            slots = pre_w2.pop(t)
                            for j in range(NJ):
                                if j + RING - 1 < NJ:
                                    slots[j + RING - 1] = load_w2(j + RING - 1)
                                r = slots[j]
                                for sub in range(4):
                                    k.op(PE, lambda e: e.matmul(PS[sub][:, :], lhsT=mT[:, j, sub * 128:(sub + 1) * 128],
                                                                rhs=WR2[r][:, :], start=(j == 0), stop=(j == NJ - 1)),
                                         [d_mT[j], d_wr2[r]], [PSD[sub]], signal=(j == NJ - 1 or sub == 3))

                        def stage2b_tail(t):
                            X, dX = XC[t % 3], d_xc[t % 3]
                            evac_y(t, 1)
                            t0 = t * 512
                            k.dma(SP, xdst[t0:t0 + 512, :].rearrange("(s p) f -> p s f", p=128), X[:],
                                  [dX], [k.ddep(xdst_name, t)], ds_xo[t % 3])

                        load_s1(0)
                        if NTILE > 1:
                            load_s1(1)
                        stage1a(0)
                        stage1b(0)
                        if NTILE > 1:
                            if NTILE > 2:
                                load_s1(2)
                            stage1a(1)
                            stage1b(1)
                        prefetch_w(0)
                        for t in range(NTILE):
                            stage2a(t)
                            evac_y(t, 0)
                            if t + 2 < NTILE:
                                stage1a(t + 2)
                            stage2b(t)
                            if t + 2 < NTILE:
                                stage1b(t + 2)
                            stage2b_tail(t)
                            if t + 3 < NTILE:
                                load_s1(t + 3, deferred=True)
                        k.barrier()
        except _Stop:
            pass
        k.dead = False
        k.final_wait(SP)
    return nc


def _host_inputs(inputs, seqs_per_core=None):
    f = lambda a: np.ascontiguousarray(np.asarray(a, dtype=np.float32))
    xp, xs = f(inputs["x_prompt"]), f(inputs["x_sample"])
    cp, cs = f(inputs["c_prompt"]), f(inputs["c_sample"])
    oh, vm = _static_tables()
    ng = np.stack([f(inputs["norm1_g"]), f(inputs["norm2_g"])], 1)
    ng = np.ascontiguousarray(ng.reshape(DEPTH, 2, 8, 128).transpose(3, 0, 1, 2))
    bada = f(inputs["b_ada"])
    badac = np.ascontiguousarray(bada.reshape(DEPTH, 48, 128).transpose(2, 0, 1))
    wada = np.ascontiguousarray(f(inputs["w_ada"]).reshape(DEPTH, 8, 128, 12, 512).transpose(0, 3, 2, 1, 4))
    win = np.ascontiguousarray(f(inputs["w_in"]).reshape(DEPTH, 8, 128, DIN).transpose(0, 2, 1, 3))
    qg, kg = f(inputs["q_norm_g"]), f(inputs["k_norm_g"])
    qkg = np.stack([np.tile(qg, (1, 2)), np.tile(kg, (1, 2))], 2)
    qkg = np.ascontiguousarray(qkg.transpose(1, 0, 2))
    cw = np.ascontiguousarray(f(inputs["conv_dw_w"]).reshape(DEPTH, CK, 4, 128).transpose(3, 0, 2, 1))
    cb = np.stack([f(inputs["conv_dw_b"]), f(inputs["conv_ln_g"]), f(inputs["conv_ln_b"])], 1)
    cb = np.ascontiguousarray(cb.reshape(DEPTH, 3, 4, 128).transpose(3, 0, 1, 2))
    wout = np.ascontiguousarray(f(inputs["w_out"]).reshape(DEPTH, 8, 128, D).transpose(0, 2, 1, 3))
    wu = f(inputs["w_up"]).reshape(DEPTH, 8, 128, 2, NJ, 128)
    wup = np.ascontiguousarray(wu.transpose(0, 4, 2, 3, 1, 5)).reshape(DEPTH, NJ, 128, 2048)
    wdn = np.ascontiguousarray(f(inputs["w_down"]).reshape(DEPTH, NJ, 128, D))
    fw = np.ascontiguousarray(f(inputs["ffn_dw_w"]).reshape(DEPTH, 3, NJ, 128).transpose(3, 0, 2, 1))
    shared = dict(relb=f(inputs["rel_bias"]), oh=oh, vm=vm, ng=ng, badac=badac, bada=bada, wada=wada, win=win,
                  qkg=qkg, cw=cw, cb=cb, wout=wout, wup=wup, wdn=wdn, fw=fw)
    in_maps = []
    for c in range(NCORES):
        xin = np.concatenate([xp[c], xs[2 * c], xs[2 * c + 1]], 0)
        cc = np.stack([cp[c], cs[2 * c], cs[2 * c + 1]], 0)
        cT = np.ascontiguousarray(cc.reshape(3, 8, 128).transpose(2, 1, 0))
        m = dict(shared)
        m["xin"] = np.ascontiguousarray(xin)
        m["cT"] = cT
        in_maps.append(m)
    return in_maps


_NC_CACHE = {}


def kernel(**inputs):
    in_maps = _host_inputs(inputs)
    if "nc" not in _NC_CACHE:
        _NC_CACHE["nc"] = build_program()
    nc = _NC_CACHE["nc"]
    res = run_bass_kernel_spmd(nc, in_maps, core_ids=list(range(NCORES)))
    ys = [np.asarray(r["y"], dtype=np.float32) for r in res.results]
    y_prompt = np.stack([ys[c][0:4096] for c in range(NCORES)], 0)
    y_sample = np.stack([ys[c // 2][4096 + 2048 * (c % 2):4096 + 2048 * (c % 2 + 1)] for c in range(2 * NCORES)], 0)
    return (y_prompt, y_sample)
```
